# Optimizing a Trainium2 kernel written in Bass

```python
import jax, jax.numpy as jnp
from jax import lax
import numpy as np

D_MODEL = 2048
BATCH = 4
SEQ = 2048
DEPTH = 4

CHUNK = 64
D_MIX = D_MODEL
D_GLA = D_MIX // 2
D_ATT = D_MIX - D_GLA
GLA_HEADS = 4
GLA_DK = D_GLA // 2 // GLA_HEADS
GLA_DV = D_GLA // GLA_HEADS
GLA_KW = GLA_HEADS * GLA_DK
GLA_GATE_RANK = 16
GLA_TAU = 16.0
ATT_HEADS = 8
ATT_HD = D_ATT // ATT_HEADS
LEFT_CHUNKS = 8
BAND = (LEFT_CHUNKS + 1) * CHUNK
REL_CLIP = 128
N_REL = 2 * REL_CLIP + 1
EPS = 1e-6

SPLIT_SIZES = (GLA_KW, GLA_KW, D_GLA, D_GLA, GLA_GATE_RANK, D_ATT, D_ATT, D_ATT, D_ATT)
D_IN = GLA_KW * 2 + D_GLA * 2 + GLA_GATE_RANK + D_ATT * 4

kernel_name = "hymba_gla_chunkattn_sandwich"


def rmsnorm(x, g):
    xf = x.astype(jnp.float32)
    y = xf * lax.rsqrt(jnp.mean(xf * xf, axis=-1, keepdims=True) + EPS) * g.astype(jnp.float32)
    return y.astype(x.dtype)


def split_columns(z):
    points = []
    acc = 0
    for s in SPLIT_SIZES[:-1]:
        acc += s
        points.append(acc)
    return jnp.split(z, points, axis=-1)


def gla_chunk_causal(q, k, v, log_a):
    out_dtype = v.dtype
    B, S, H, DK = q.shape
    DV = v.shape[-1]
    nc = S // CHUNK
    qf = q.astype(jnp.float32).reshape(B, nc, CHUNK, H, DK) * (DK ** -0.5)
    kf = k.astype(jnp.float32).reshape(B, nc, CHUNK, H, DK)
    vf = v.astype(jnp.float32).reshape(B, nc, CHUNK, H, DV)
    L = jnp.cumsum(log_a.astype(jnp.float32).reshape(B, nc, CHUNK, H, DK), axis=2)
    L_end = L[:, :, -1]
    k_dec = kf * jnp.exp(L_end[:, :, None] - L)
    U = jnp.einsum('bnchk,bnchv->bnhkv', k_dec, vf)
    A = jnp.exp(L_end)

    def step(state, inp):
        a, u = inp
        new = a[..., None] * state + u
        return new, new

    init = jnp.zeros((B, H, DK, DV), jnp.float32)
    _, states = lax.scan(step, init, (jnp.swapaxes(A, 0, 1), jnp.swapaxes(U, 0, 1)))
    states = jnp.swapaxes(states, 0, 1)
    o = jnp.einsum('bnchk,bnhkv->bnchv', qf, states)
    return o.reshape(B, S, H, DV).astype(out_dtype)


def chunk_band_attention(q, k, v, rel_bias):
    B, S, H, D = q.shape
    nc = S // CHUNK
    qc = q.reshape(B, nc, CHUNK, H, D)
    pad = ((0, 0), (LEFT_CHUNKS * CHUNK, 0), (0, 0), (0, 0))
    kp = jnp.pad(k, pad).reshape(B, nc + LEFT_CHUNKS, CHUNK, H, D)
    vp = jnp.pad(v, pad).reshape(B, nc + LEFT_CHUNKS, CHUNK, H, D)
    band_idx = jnp.arange(nc)[:, None] + jnp.arange(LEFT_CHUNKS + 1)[None, :]
    kb = kp[:, band_idx].reshape(B, nc, BAND, H, D)
    vb = vp[:, band_idx].reshape(B, nc, BAND, H, D)
    scores = jnp.einsum('bnqhd,bnkhd->bnhqk', qc, kb,
                        preferred_element_type=jnp.float32) * (D ** -0.5)
    qi = jnp.arange(CHUNK)[:, None] + LEFT_CHUNKS * CHUNK
    kj = jnp.arange(BAND)[None, :]
    rel = jnp.clip(qi - kj, -REL_CLIP, REL_CLIP) + REL_CLIP
    bias = rel_bias.astype(jnp.float32)[:, rel]
    key_chunk = band_idx - LEFT_CHUNKS
    valid = jnp.repeat(key_chunk >= 0, CHUNK, axis=1)
    scores = jnp.where(valid[None, :, None, None, :], scores + bias[None, None], -jnp.inf)
    p = jax.nn.softmax(scores, axis=-1).astype(v.dtype)
    o = jnp.einsum('bnhqk,bnkhd->bnqhd', p, vb)
    return o.reshape(B, S, H, D)


def hybrid_layer(x, w_in, w_out, g_pre, g_post, w_alpha, b_alpha, g_gla, g_att, rel_bias):
    B, S, _ = x.shape
    h = rmsnorm(x, g_pre)
    z = h @ w_in
    gq, gk, gv, gg, ga, aq, ak, av, ag = split_columns(z)
    log_a = jax.nn.log_sigmoid((ga @ w_alpha + b_alpha).astype(jnp.float32)) / GLA_TAU
    o_gla = gla_chunk_causal(gq.reshape(B, S, GLA_HEADS, GLA_DK),
                             gk.reshape(B, S, GLA_HEADS, GLA_DK),
                             gv.reshape(B, S, GLA_HEADS, GLA_DV),
                             log_a.reshape(B, S, GLA_HEADS, GLA_DK))
    o_gla = rmsnorm(o_gla, g_gla.reshape(GLA_HEADS, GLA_DV)).reshape(B, S, D_GLA)
    o_gla = o_gla * jax.nn.silu(gg)
    o_att = chunk_band_attention(aq.reshape(B, S, ATT_HEADS, ATT_HD),
                                 ak.reshape(B, S, ATT_HEADS, ATT_HD),
                                 av.reshape(B, S, ATT_HEADS, ATT_HD), rel_bias)
    o_att = rmsnorm(o_att, g_att.reshape(ATT_HEADS, ATT_HD)).reshape(B, S, D_ATT)
    o_att = o_att * jax.nn.silu(ag)
    y = jnp.concatenate([o_gla, o_att], axis=-1) @ w_out
    return x + rmsnorm(y, g_post)


def setup_inputs(seed: int = 0) -> dict:
    key = jax.random.key(seed)
    ks = jax.random.split(key, 10)
    x = jax.random.normal(ks[0], (BATCH, SEQ, D_MODEL), jnp.float32)
    w_in = jax.random.normal(ks[1], (DEPTH, D_MODEL, D_IN), jnp.float32) * (D_MODEL ** -0.5)
    w_out = jax.random.normal(ks[2], (DEPTH, D_MIX, D_MODEL), jnp.float32) * (D_MIX ** -0.5)
    g_pre = 1.0 + 0.02 * jax.random.normal(ks[3], (DEPTH, D_MODEL), jnp.float32)
    g_post = 1.0 + 0.02 * jax.random.normal(ks[4], (DEPTH, D_MODEL), jnp.float32)
    w_alpha = jax.random.normal(ks[5], (DEPTH, GLA_GATE_RANK, GLA_KW), jnp.float32) * (GLA_GATE_RANK ** -0.5)
    b_alpha = 0.1 * jax.random.normal(ks[6], (DEPTH, GLA_KW), jnp.float32)
    g_gla = 1.0 + 0.02 * jax.random.normal(ks[7], (DEPTH, D_GLA), jnp.float32)
    g_att = 1.0 + 0.02 * jax.random.normal(ks[8], (DEPTH, D_ATT), jnp.float32)
    rel_bias = 0.1 * jax.random.normal(ks[9], (DEPTH, ATT_HEADS, N_REL), jnp.float32)
    return {"x": x, "w_in": w_in, "w_out": w_out, "g_pre": g_pre, "g_post": g_post,
            "w_alpha": w_alpha, "b_alpha": b_alpha, "g_gla": g_gla, "g_att": g_att,
            "rel_bias": rel_bias}


def reference(x, w_in, w_out, g_pre, g_post, w_alpha, b_alpha, g_gla, g_att, rel_bias):
    h = x
    for l in range(DEPTH):
        h = hybrid_layer(h, w_in[l], w_out[l], g_pre[l], g_post[l], w_alpha[l], b_alpha[l],
                         g_gla[l], g_att[l], rel_bias[l])
    return h
```

```python
import contextlib
import math
import os
import numpy as np
import concourse.bass as bass
import concourse.mybir as mybir
from concourse.bass_utils import run_bass_kernel_spmd

F32 = mybir.dt.float32
BF16 = mybir.dt.bfloat16
AF = mybir.ActivationFunctionType
ALU = mybir.AluOpType

EPOCH = 30000
ENGINES = ("sync", "gpsimd", "scalar", "vector", "tensor")
SAFE_SELF = ("tensor",)

DEPTH = 4
DM = 2048
SEQ = 2048
NKT = 16
EPS = 1e-6
NG = 44
NHA = 4
NHG = 2
NBLK = 28
GROUPS = [[0, 1], [2, 3], [4, 5], [6, 7]]
SCALE_ATT = 128.0 ** -0.5
SCALE_GLA = 128.0 ** -0.5
NEG = -30000.0


class Op:
    __slots__ = ("eng", "fn", "reads", "writes", "slot", "ndma", "waits", "tok", "bar", "sinc")

    def __init__(self, eng, fn, reads, writes, slot, ndma, bar=False, sinc=16):
        self.sinc = sinc
        self.eng = eng
        self.fn = fn
        self.reads = tuple(reads)
        self.writes = tuple(writes)
        self.slot = slot
        self.ndma = ndma
        self.waits = []
        self.tok = None
        self.bar = bar


class Prog:
    def __init__(self, nc):
        self.nc = nc
        self.ops = []

    def add(self, eng, fn, reads=(), writes=(), slot=None, ndma=1, sinc=16):
        self.ops.append(Op(eng, fn, reads, writes, slot, ndma, sinc=sinc))

    def mark(self):
        self.ops.append(None)

    def capture(self, fn):
        n0 = len(self.ops)
        fn()
        got = self.ops[n0:]
        del self.ops[n0:]
        return got

    def merge(self, xs, ys):
        def chunks(lst):
            out, cur = [], []
            for o in lst:
                if o is None:
                    if cur:
                        out.append(cur)
                    cur = []
                else:
                    cur.append(o)
            if cur:
                out.append(cur)
            return out
        cx, cy = chunks(xs), chunks(ys)
        for i in range(max(len(cx), len(cy))):
            if i < len(cx):
                self.ops.extend(cx[i])
            if i < len(cy):
                self.ops.extend(cy[i])

    def barrier(self):
        for e in ENGINES:
            self.ops.append(Op(e, None, (), (), None, 1, bar=True))

    def finalize(self, stack):
        nc = self.nc
        self.ops = [o for o in self.ops if o is not None]
        state = {}
        cnt = {}
        latest = {}
        known = {e: {} for e in ENGINES}
        ecount = {e: 0 for e in ENGINES}
        keys = []
        for op in self.ops:
            waits = {}

            def need(tok, op=op, waits=waits):
                if tok is None:
                    return
                k, v = tok
                if k[0] == "E" and k[1] == op.eng and op.eng in SAFE_SELF:
                    return
                if known[op.eng].get(k, 0) < v and waits.get(k, 0) < v:
                    waits[k] = v

            if op.bar:
                for k, v in latest.items():
                    if k[0] == "D" and (k[1].startswith("cc") or k[1].startswith("wb")):
                        continue
                    need((k, v))
            for r in op.reads:
                st = state.get(r)
                if st is not None:
                    need(st[0])
            for w in op.writes:
                st = state.get(w)
                if st is not None:
                    need(st[0])
                    for k, v in st[1].items():
                        need((k, v))
            for k, v in waits.items():
                known[op.eng][k] = v
            op.waits = list(waits.items())
            if op.fn is None:
                continue
            if op.slot is not None:
                key = ("D", op.slot)
                cnt[key] = cnt.get(key, 0) + op.sinc * op.ndma
                tok = (key, cnt[key])
            else:
                ecount[op.eng] += 1
                ep = (ecount[op.eng] - 1) // EPOCH
                key = ("E", op.eng, ep)
                tok = (key, ecount[op.eng] - ep * EPOCH)
            if key not in latest:
                keys.append(key)
            latest[key] = tok[1]
            op.tok = tok
            for r in op.reads:
                st = state.setdefault(r, [None, {}])
                if st[1].get(tok[0], 0) < tok[1]:
                    st[1][tok[0]] = tok[1]
            for w in op.writes:
                state[w] = [tok, {}]
        sems = {}
        for i, key in enumerate(keys):
            sems[key] = stack.enter_context(nc.semaphore("s%d" % i))
        self.nsems = len(keys)
        block = stack.enter_context(nc.Block())
        for ename in ENGINES:
            eops = [op for op in self.ops if op.eng == ename]
            if not eops:
                continue

            def body(e, eops=eops):
                for op in eops:
                    for k, v in op.waits:
                        e.wait_ge(sems[k], v)
                    if op.fn is None:
                        continue
                    ins = op.fn(e)
                    if op.slot is not None:
                        if not isinstance(ins, (list, tuple)):
                            ins = [ins]
                        assert len(ins) == op.ndma, (len(ins), op.ndma)
                        for i_ in ins:
                            i_.then_inc(sems[op.tok[0]], op.sinc)
                    else:
                        ins.then_inc(sems[op.tok[0]], 1)

            getattr(block, ename)(body)


def build(depth, stop=None):
    nc = bass.Bass("TRN2", target_bir_lowering=False)
    xT_d = nc.dram_tensor("xT", [DM, SEQ], F32, kind="ExternalInput").ap()
    wmain_d = nc.dram_tensor("w_main", [depth * NBLK, 128, 2048], F32, kind="ExternalInput").ap()
    wga_d = nc.dram_tensor("w_ga", [depth, 128, 256], F32, kind="ExternalInput").ap()
    wout_d = nc.dram_tensor("w_outr", [depth * 16, 128, 2048], F32, kind="ExternalInput").ap()
    gains_d = nc.dram_tensor("gains", [128, depth * NG], F32, kind="ExternalInput").ap()
    wal_d = nc.dram_tensor("wal", [depth, 32, 256], F32, kind="ExternalInput").ap()
    biasT_d = nc.dram_tensor("biasT", [depth * NHA, 128, 256], F32, kind="ExternalInput").ap()
    outT_d = nc.dram_tensor("outT", [DM, SEQ], F32, kind="ExternalOutput").ap()
    XS_d = nc.dram_tensor("xs_scratch", [DM, SEQ], F32).ap() if depth > 1 else None
    OCam_d = nc.dram_tensor("oca_mine", [512, SEQ], BF16).ap()
    OCgm_d = [nc.dram_tensor("ocg_mine%d" % i, [256, SEQ], BF16).ap() for i in range(NHG)]
    OCaf_d = nc.dram_tensor("oca_full", [1024, SEQ], BF16).ap()
    OCgf_d = [nc.dram_tensor("ocg_full%d" % i, [512, SEQ], BF16).ap() for i in range(NHG)]

    with contextlib.ExitStack() as st:
        NW = 52480
        arena = st.enter_context(nc.sbuf_tensor("arena", [128, NW], F32))
        ps = [st.enter_context(nc.psum_tensor("ps%d" % i, [128, 512], F32)) for i in range(8)]
        P = Prog(nc)

        def V(off, n, dt=F32, p0=0, p1=128):
            if dt == F32:
                return arena[p0:p1, off:off + n]
            assert n % 2 == 0
            return arena[p0:p1, off:off + n // 2].bitcast(BF16)

        pos = [0]

        def A(nbytes):
            o = pos[0]
            pos[0] += (nbytes + 3) // 4
            return o

        o_RH = A(16 * 2048 * 2)
        HT = [V(o_RH + j * 1024, 2048, BF16) for j in range(16)]
        NWB = 4
        o_WB = [A(4096) for _ in range(NWB)]
        WB = [V(o, 2048, BF16) for o in o_WB]
        ones = V(A(256), 128, BF16)
        ident = V(A(256), 128, BF16)
        triU = V(A(512), 128)
        cind = V(A(16), 2)
        gains = V(A(depth * NG * 4), depth * NG)
        o_wal = A(1024)
        wal = V(o_wal, 256, F32, 0, 32)
        biasb = [V(A(1024), 256) for _ in range(2)]
        wgas = V(A(1024), 256)
        wgab = V(A(512), 256, BF16)
        tnh = V(A(2048), 512)
        identf = tnh[:, 0:128]
        o_RM = pos[0]
        RM_BYTES = (NW - o_RM) * 4
        E_WORDS = 6 * 1024
        o_E = NW - E_WORDS
        gqT = [V(o_E, 2048, BF16), None]
        gkT = V(o_E + 1024, 2048, BF16)
        gvT = [V(o_E + 2048, 2048, BF16), V(o_E + 3072, 2048, BF16)]
        ggf = [[V(o_E + 4096, 2048, BF16), V(o_E + 5120, 2048, BF16)], [None, None]]

        def gla_proj(h):
            hb = h % 2
            project(ev_copy("scalar", gqT[hb], "gqT%d" % hb), mark=True)
            project(ev_copy("vector", gkT, "gkT"), mark=True)
            project(ev_copy("vector", gvT[0], "gvT0"), mark=True)
            project(ev_copy("scalar", gvT[1], "gvT1"), mark=True)
            project(ev_silu(ggf[hb][0], "ggf%d_0" % hb), mark=True)
            project(ev_silu(ggf[hb][1], "ggf%d_1" % hb), mark=True)

        def gcol(l, k):
            c = l * NG + k
            return gains[:, c:c + 1]

        sched = []
        for l in range(depth):
            for a in range(NHA):
                for t in (0, 4, 8, 12):
                    sched.append(wmain_d[l * NBLK + t + a])
            for h in range(NHG):
                for t in (16 + h, 18 + h, 20 + 2 * h, 21 + 2 * h, 24 + 2 * h, 25 + 2 * h):
                    sched.append(wmain_d[l * NBLK + t])
            for s in range(4):
                for j in range(16):
                    sched.append(wout_d[l * 16 + j])
        wstate = {"loaded": 0, "used": 0}

        def emit_load(n):
            src = sched[n]
            j = n % NWB
            P.add("gpsimd", lambda e, j=j, src=src: e.dma_start(out=WB[j], in_=src), writes=["wb%d" % j], slot="wb%d" % j)

        def next_block():
            n = wstate["used"]
            wstate["used"] += 1
            while wstate["loaded"] < min(n + NWB, len(sched)):
                emit_load(wstate["loaded"])
                wstate["loaded"] += 1
            return n % NWB

        acc = [0]
        HTR = ["hT%d.%d" % (j, s_) for j in range(16) for s_ in range(4)]

        def project(evac, src_res=HTR, M=128, wb=None, wres=None, mark=False):
            if wb is None:
                j = next_block()
                wbv = WB[j]
                wres = "wb%d" % j
                stride = 128
            else:
                wbv = wb
                stride = M
            for s in range(4):
                bank = acc[0]
                acc[0] ^= 1

                def mm(e, s=s, bank=bank, wbv=wbv, stride=stride, k0=0, k1=NKT):
                    for kt in range(k0, k1):
                        ins = e.matmul(ps[bank][0:M, :], lhsT=wbv[:, kt * stride:kt * stride + M],
                                       rhs=HT[kt][:, s * 512:(s + 1) * 512], start=(kt == 0), stop=(kt == NKT - 1))
                    return ins

                if mark:
                    P.add("tensor", lambda e, mm=mm: mm(e, k0=0, k1=NKT // 2), reads=[wres] + list(src_res), writes=["pb%d" % bank])
                    P.mark()
                    P.add("tensor", lambda e, mm=mm: mm(e, k0=NKT // 2, k1=NKT), reads=[wres] + list(src_res), writes=["pb%d" % bank])
                else:
                    P.add("tensor", mm, reads=[wres] + list(src_res), writes=["pb%d" % bank])
                evac(s, ps[bank], "pb%d" % bank)
                if mark:
                    P.mark()

        def ev_copy(eng, dst, dres):
            def f(s, pb, pres):
                if eng == "scalar":
                    P.add("scalar", lambda e, s=s, pb=pb: e.copy(out=dst[:, s * 512:(s + 1) * 512], in_=pb[:, :]),
                          reads=[pres], writes=[dres])
                else:
                    P.add("vector", lambda e, s=s, pb=pb: e.tensor_copy(out=dst[:, s * 512:(s + 1) * 512], in_=pb[:, :]),
                          reads=[pres], writes=[dres])
            return f

        def ev_silu(dst, dres):
            def f(s, pb, pres):
                P.add("scalar", lambda e, pb=pb: e.activation(out=tnh, in_=pb[:, :], func=AF.Tanh, scale=0.5),
                      reads=[pres], writes=["tnh"])
                P.add("vector", lambda e, s=s, pb=pb: e.scalar_tensor_tensor(out=dst[:, s * 512:(s + 1) * 512], in0=tnh, scalar=1.0,
                                                                            op0=ALU.add, in1=pb[:, :], op1=ALU.mult),
                      reads=[pres, "tnh"], writes=[dres])
            return f

        P.add("sync", lambda e: e.dma_start(out=gains, in_=gains_d), writes=["gains"], slot="gains")
        P.add("gpsimd", lambda e: e.memset(ones, 1.0), writes=["ones"])
        P.add("gpsimd", lambda e: e.memset(identf, 1.0), writes=["identf"])
        P.add("gpsimd", lambda e: e.affine_select(out=identf, in_=identf, pattern=[[1, 128]], compare_op=ALU.is_equal,
                                                  fill=0.0, base=0, channel_multiplier=-1), reads=["identf"], writes=["identf"])
        P.add("gpsimd", lambda e: e.tensor_copy(out=ident, in_=identf), reads=["identf"], writes=["ident"])
        P.add("gpsimd", lambda e: e.memset(triU, 1.0), writes=["triU"])
        P.add("gpsimd", lambda e: e.affine_select(out=triU, in_=triU, pattern=[[-1, 128]], compare_op=ALU.is_gt,
                                                  fill=0.0, base=0, channel_multiplier=1), reads=["triU"], writes=["triU"])
        P.add("gpsimd", lambda e: e.memset(triU[64:128, 0:64], 0.0), reads=["triU"], writes=["triU"])
        P.add("gpsimd", lambda e: e.memset(cind, 0.0), writes=["cind"])
        P.add("gpsimd", lambda e: e.memset(cind[0:64, 0:1], 1.0), reads=["cind"], writes=["cind"])
        P.add("gpsimd", lambda e: e.memset(cind[64:128, 1:2], 1.0), reads=["cind"], writes=["cind"])

        XR = [["X%d.%d" % (j, s) for s in range(4)] for j in range(16)]

        def xview(dram, s0, n):
            return dram.rearrange("(j p) t -> p j t", p=128)[:, :, s0:s0 + n]

        for l in range(depth):
            src_d = xT_d if l == 0 else XS_d
            dst_d = outT_d if l == depth - 1 else XS_d

            pos[0] = o_RM
            NXS = 4
            xs = [V(A(16 * 256 * 4), 4096) for _ in range(NXS)]
            sqp = [V(A(16 * 256 * 2), 4096, BF16) for _ in range(NXS)]
            rsp = [V(A(1024), 256) for _ in range(NXS)]
            assert (pos[0] - o_RM) * 4 <= RM_BYTES
            for s in (range(8) if l == 0 else []):
                b = s % NXS
                xr = [XR[j][s // 2] for j in range(16)]
                P.add("sync", lambda e, b=b, s=s, src_d=src_d: e.dma_start(
                    out=xs[b].rearrange("p (j t) -> p j t", j=16), in_=xview(src_d, s * 256, 256)),
                    reads=xr, writes=["xs%d" % b], slot="xs%d" % b)
                P.add("scalar", lambda e, b=b: e.activation(out=sqp[b], in_=xs[b], func=AF.Square),
                      reads=["xs%d" % b], writes=["sqp%d" % b])
                half = "pb%d" % (4 + b)
                msap = ps[4 + b][:, 0:256]

                def mmp(e, b=b, msap=msap):
                    for j in range(16):
                        ins = e.matmul(msap, lhsT=ones, rhs=sqp[b][:, j * 256:(j + 1) * 256], start=(j == 0), stop=(j == 15))
                    return ins
                P.add("tensor", mmp, reads=["sqp%d" % b, "ones"], writes=[half])
                P.add("scalar", lambda e, b=b, msap=msap: e.activation(out=rsp[b], in_=msap, func=AF.Ln, bias=EPS, scale=1.0 / DM),
                      reads=[half], writes=["rsp%d" % b])
                P.add("scalar", lambda e, b=b: e.activation(out=rsp[b], in_=rsp[b], func=AF.Exp, scale=-0.5),
                      reads=["rsp%d" % b], writes=["rsp%d" % b])
                for j in range(16):
                    P.add("vector", lambda e, b=b, j=j, s=s, l=l: e.scalar_tensor_tensor(
                        out=HT[j][:, s * 256:(s + 1) * 256], in0=xs[b][:, j * 256:(j + 1) * 256], scalar=gcol(l, j),
                        op0=ALU.mult, in1=rsp[b], op1=ALU.mult),
                        reads=["xs%d" % b, "rsp%d" % b, "gains"], writes=["hT%d.%d" % (j, s // 2)])
            P.barrier()
            if stop == "P":
                break

            pos[0] = o_RM
            qT = [V(A(4096), 2048, BF16) for _ in range(2)]
            kT = [V(A(4096), 2048, BF16) for _ in range(2)]
            vT = V(A(4096), 2048, BF16)
            vtok = [V(A(4096), 2048, BF16) for _ in range(2)]
            gatef = [V(A(4096), 2048, BF16) for _ in range(3)]
            od = [V(A(16384), 4096) for _ in range(2)]
            PT = [V(A(1280), 640, BF16) for _ in range(2)]
            tmpn = [V(A(1024), 256) for _ in range(2)]
            sqa = V(A(4096), 2048, BF16)
            t1 = V(A(8192), 2048)
            ofin = V(A(4096), 2048, BF16)
            assert pos[0] <= o_E, (pos[0], o_E)
            for par in range(2):
                P.add("gpsimd", lambda e, par=par: e.memset(PT[par][0:64, 576:640], 0.0), writes=["PTz%d" % par])

            def att_norm(a, l=l):
                ob = a % 2
                odv = od[ob].rearrange("p (m c t) -> p m c t", m=16, c=2)
                ov = odv[:, :, 0, :]
                dv = odv[:, :, 1, :]
                odr = "od%d" % ob
                P.add("scalar", lambda e: e.activation(out=sqa.rearrange("p (m t) -> p m t", m=16), in_=ov, func=AF.Square),
                      reads=[odr], writes=["sqa"])
                P.add("scalar", lambda e: e.activation(out=t1.rearrange("p (m t) -> p m t", m=16), in_=dv, func=AF.Square,
                                                       scale=math.sqrt(EPS)), reads=[odr], writes=["t1"])
                P.mark()
                for s in range(4):
                    bank = acc[0]
                    acc[0] ^= 1
                    P.add("tensor", lambda e, s=s, bank=bank: e.matmul(ps[bank][:, :], lhsT=ones, rhs=sqa[:, s * 512:(s + 1) * 512],
                                                                       start=True, stop=True),
                          reads=["sqa", "ones"], writes=["pb%d" % bank])
                    P.add("vector", lambda e, s=s, bank=bank: e.scalar_tensor_tensor(
                        out=t1[:, s * 512:(s + 1) * 512], in0=ps[bank][:, :], scalar=1.0 / 128.0, op0=ALU.mult,
                        in1=t1[:, s * 512:(s + 1) * 512], op1=ALU.add),
                        reads=["pb%d" % bank, "t1"], writes=["t1"])
                    P.mark()
                P.add("scalar", lambda e: e.activation(out=t1, in_=t1, func=AF.Ln), reads=["t1"], writes=["t1"])
                P.add("scalar", lambda e: e.activation(out=t1, in_=t1, func=AF.Exp, scale=-0.5, bias=math.log(0.5)),
                      reads=["t1"], writes=["t1"])
                P.mark()
                P.add("vector", lambda e: e.tensor_tensor(out=ov, in0=ov, in1=t1.rearrange("p (m t) -> p m t", m=16), op=ALU.mult),
                      reads=[odr, "t1"], writes=[odr])
                P.add("vector", lambda e, a=a: e.scalar_tensor_tensor(
                    out=ofin.rearrange("p (m t) -> p m t", m=16), in0=ov, scalar=gcol(l, 32 + a), op0=ALU.mult,
                    in1=gatef[a % 3].rearrange("p (m t) -> p m t", m=16), op1=ALU.mult),
                    reads=[odr, "gains", "gatef%d" % (a % 3)], writes=["ofin"])
                P.add("sync", lambda e, a=a: e.dma_start(out=OCam_d[a * 128:(a + 1) * 128, :], in_=ofin),
                      reads=["ofin"], writes=["OCam"], slot="ofin")
                P.mark()

            def att_proj(a, l=l):
                ob = a % 2
                bb = biasb[ob]
                bres = "biasb%d" % ob
                P.add("sync", lambda e: e.dma_start(out=bb, in_=biasT_d[l * NHA + a]), writes=[bres], slot=bres)
                P.add("gpsimd", lambda e: e.memset(bb[64:128, 0:64], NEG), reads=[bres], writes=[bres])
                project(ev_copy("vector", qT[ob], "qT%d" % ob), mark=True)
                project(ev_copy("vector", kT[ob], "kT%d" % ob), mark=True)
                project(ev_copy("vector", vT, "vT"), mark=True)
                project(ev_silu(gatef[a % 3], "gatef%d" % (a % 3)), mark=True)
                for hb in range(2):
                    bank = acc[0]
                    acc[0] ^= 1
                    pv = ps[bank][:, :].bitcast(BF16)

                    def tr(e, hb=hb, pv=pv):
                        for i in range(8):
                            tt = hb * 8 + i
                            ins = e.transpose(pv[:, i * 128:(i + 1) * 128], vT[:, tt * 128:(tt + 1) * 128], ident)
                        return ins
                    P.add("tensor", tr, reads=["vT", "ident"], writes=["pb%d" % bank])
                    P.add("vector", lambda e, hb=hb, pv=pv, bank=bank: e.tensor_copy(out=vtok[ob][:, hb * 1024:(hb + 1) * 1024], in_=pv),
                          reads=["pb%d" % bank], writes=["vtok%d" % ob])
                    P.mark()

            def att_core(a, l=l):
                ob = a % 2
                bb = biasb[ob]
                bres = "biasb%d" % ob
                bfar = gcol(l, 40 + a)
                qTa, kTa, vtoka = qT[ob], kT[ob], vtok[ob]
                qr, kr, vr = "qT%d" % ob, "kT%d" % ob, "vtok%d" % ob

                def slots(m):
                    return [(m - i, i) for i in range(5) if m - i >= 0]

                def emit_scores(m):
                    par = m % 2
                    bA, bB = 2 + 2 * par, 3 + 2 * par

                    def sc(e):
                        for kt, sl in slots(m):
                            dst = ps[bA][:, sl * 128:(sl + 1) * 128] if sl < 4 else ps[bB][:, 0:128]
                            ins = e.matmul(dst, lhsT=kTa[:, kt * 128:(kt + 1) * 128], rhs=qTa[:, m * 128:(m + 1) * 128],
                                           start=True, stop=True)
                        return ins
                    P.add("tensor", sc, reads=[kr, qr], writes=["pb%d" % bA, "pb%d" % bB])

                def emit_soft(m):
                    par = m % 2
                    bA, bB = 2 + 2 * par, 3 + 2 * par
                    nc_ = 256 if m >= 1 else 128
                    P.add("vector", lambda e: e.scalar_tensor_tensor(
                        out=tmpn[par][:, 0:nc_], in0=ps[bA][:, 0:nc_], scalar=SCALE_ATT, op0=ALU.mult,
                        in1=bb[:, 0:nc_], op1=ALU.add), reads=["pb%d" % bA, bres], writes=["tmpn%d" % par])
                    P.add("scalar", lambda e: e.activation(out=PT[par][:, 0:nc_], in_=tmpn[par][:, 0:nc_], func=AF.Exp),
                          reads=["tmpn%d" % par], writes=["PT%dn" % par])
                    if m >= 2:
                        w = 256 if m >= 3 else 128
                        P.add("scalar", lambda e: e.activation(out=PT[par][:, 256:256 + w], in_=ps[bA][:, 256:256 + w], func=AF.Exp,
                                                               bias=bfar, scale=SCALE_ATT),
                              reads=["pb%d" % bA, "gains"], writes=["PT%df" % par])
                    if m >= 4:
                        P.add("scalar", lambda e: e.activation(out=PT[par][0:64, 512:576], in_=ps[bB][0:64, 0:64], func=AF.Exp,
                                                               bias=bfar[0:64, :], scale=SCALE_ATT),
                              reads=["pb%d" % bB, "gains"], writes=["PT%dc" % par])
                        P.add("scalar", lambda e: e.activation(out=PT[par][64:128, 512:640], in_=ps[bB][64:128, 0:128], func=AF.Exp,
                                                               bias=bfar[64:128, :], scale=SCALE_ATT),
                              reads=["pb%d" % bB, "gains"], writes=["PT%dd" % par])

                def emit_pv(m):
                    par = m % 2
                    half = "pb%d" % (6 + par)
                    oacc = ps[6 + par][:, 0:128]
                    dacc = ps[6 + par][:, 128:256]

                    def pvf(e):
                        sl = slots(m)
                        for n, (kt, s_) in enumerate(sl):
                            e.matmul(oacc, lhsT=vtoka[:, kt * 128:(kt + 1) * 128], rhs=PT[par][:, s_ * 128:(s_ + 1) * 128],
                                     start=(n == 0), stop=(n == len(sl) - 1))
                        for n, (kt, s_) in enumerate(sl):
                            ins = e.matmul(dacc, lhsT=ones, rhs=PT[par][:, s_ * 128:(s_ + 1) * 128],
                                           start=(n == 0), stop=(n == len(sl) - 1))
                        return ins
                    P.add("tensor", pvf, reads=["PT%dn" % par, "PT%df" % par, "PT%dc" % par, "PT%dd" % par, "PTz%d" % par, vr, "ones"], writes=[half])
                    P.add("vector", lambda e: e.tensor_copy(out=od[ob][:, m * 256:(m + 1) * 256], in_=ps[6 + par][:, 0:256]),
                          reads=[half], writes=["od%d" % ob])

                emit_scores(0)
                for m in range(16):
                    emit_soft(m)
                    if m + 1 < 16:
                        emit_scores(m + 1)
                    P.mark()
                    emit_pv(m)
                    P.mark()

            att_proj(0)
            for a in range(NHA):
                xs_ = P.capture(lambda: att_core(a))

                def ystream(a=a):
                    if a + 1 < NHA:
                        att_proj(a + 1)
                    else:
                        gla_proj(0)
                    if a >= 1:
                        att_norm(a - 1)
                ys_ = P.capture(ystream)
                P.merge(xs_, ys_)
            att_norm(NHA - 1)
            P.add("gpsimd", lambda e: e.collective_compute("AllGather", ALU.bypass, replica_groups=GROUPS,
                                                           ins=[OCam_d], outs=[OCaf_d]),
                  reads=["OCam"], writes=["OCaf"], slot="cca", sinc=1)
            P.add("sync", lambda e, l=l: e.dma_start(out=wgas, in_=wga_d[l]), writes=["wgas"], slot="wgas")
            P.add("vector", lambda e: e.tensor_copy(out=wgab, in_=wgas), reads=["wgas"], writes=["wgab"])
            P.add("gpsimd", lambda e: e.memset(wal, 0.0), writes=["wal"])
            P.add("sync", lambda e, l=l: e.dma_start(out=wal, in_=wal_d[l]), writes=["wal"], slot="wal")
            P.barrier()
            if stop == "A":
                break

            pos[0] = o_RM
            gqT[1] = V(A(4096), 2048, BF16)
            ggf[1] = [V(A(4096), 2048, BF16) for _ in range(2)]
            o_ga = A(8192)
            gaT = V(o_ga, 2048, F32, 0, 32)
            nla = V(A(8192), 2048)
            dec = V(A(8192), 2048)
            kdec = V(A(4096), 2048, BF16)
            gvtok = V(A(8192), 4096, BF16)
            Sf = [V(A(1024), 256) for _ in range(2)]
            Sb = [V(A(512), 256, BF16) for _ in range(2)]
            Aexp = V(A(128), 32)
            oun = [V(A(4096), 1024) for _ in range(2)]
            sqg = [V(A(2048), 1024, BF16) for _ in range(2)]
            rsg = [V(A(2048), 512) for _ in range(2)]
            ofg = [V(A(2048), 1024, BF16) for _ in range(2)]
            assert pos[0] <= o_E, (pos[0], o_E)
            P.add("gpsimd", lambda e: e.memset(gaT, 1.0), writes=["gaT"])

            def ev_ga(s, pb, pres):
                P.add("scalar", lambda e, s=s, pb=pb: e.copy(out=gaT[0:16, s * 512:(s + 1) * 512], in_=pb[0:16, :]),
                      reads=[pres], writes=["gaT"])
            project(ev_ga, M=16, wb=wgab, wres="wgab")

            def gate_prep(h):
                for q4 in range(4):
                    bank = 2 + q4
                    bres_ = ["pb%d" % bank]

                    def pre(e, q4=q4, bank=bank):
                        for i in range(4):
                            tt = q4 * 4 + i
                            ins = e.matmul(ps[bank][:, i * 128:(i + 1) * 128], lhsT=gaT[:, tt * 128:(tt + 1) * 128],
                                           rhs=wal[:, h * 128:(h + 1) * 128], start=True, stop=True)
                        return ins
                    P.add("tensor", pre, reads=["gaT", "wal"], writes=bres_)
                    sl = slice(q4 * 512, (q4 + 1) * 512)
                    P.add("scalar", lambda e, bank=bank, sl=sl: e.activation(out=nla[:, sl], in_=ps[bank][:, :], func=AF.Exp, scale=-1.0),
                          reads=bres_, writes=["nla%d" % q4])
                    P.add("scalar", lambda e, sl=sl: e.activation(out=nla[:, sl], in_=nla[:, sl], func=AF.Ln, bias=1.0, scale=1.0),
                          reads=["nla%d" % q4], writes=["nla%d" % q4])

                    def dmm(e, q4=q4, bank=bank):
                        for i in range(4):
                            tt = q4 * 4 + i
                            ins = e.matmul(ps[bank][:, i * 128:(i + 1) * 128], lhsT=triU, rhs=nla[:, tt * 128:(tt + 1) * 128],
                                           start=True, stop=True)
                        return ins
                    P.add("tensor", dmm, reads=["nla%d" % q4, "triU"], writes=bres_)
                    P.add("scalar", lambda e, bank=bank, sl=sl: e.activation(out=dec[:, sl], in_=ps[bank][:, :], func=AF.Exp, scale=-1.0 / 16.0),
                          reads=bres_, writes=["dec%d" % q4])

                def lend(e):
                    for tt in range(16):
                        ins = e.matmul(ps[7][:, tt * 2:tt * 2 + 2], lhsT=nla[:, tt * 128:(tt + 1) * 128], rhs=cind, start=True, stop=True)
                    return ins
                P.add("tensor", lend, reads=["nla0", "nla1", "nla2", "nla3", "cind"], writes=["pb7"])
                P.add("scalar", lambda e: e.activation(out=Aexp, in_=ps[7][:, 0:32], func=AF.Exp, scale=-1.0 / 16.0),
                      reads=["pb7"], writes=["Aexp"])
                for hb in range(2):
                    bank = 6 + hb
                    pk = ps[bank][:, :].bitcast(BF16)
                    bres_ = ["pb%d" % bank]

                    def trk(e, hb=hb, pk=pk):
                        for i in range(8):
                            tt = hb * 8 + i
                            ins = e.transpose(pk[:, i * 128:(i + 1) * 128], gkT[:, tt * 128:(tt + 1) * 128], ident)
                        return ins
                    P.add("tensor", trk, reads=["gkT", "ident"], writes=bres_)
                    P.add("vector", lambda e, hb=hb, pk=pk: e.tensor_tensor(
                        out=kdec[:, hb * 1024:(hb + 1) * 1024], in0=pk, in1=dec[:, hb * 1024:(hb + 1) * 1024], op=ALU.mult),
                        reads=bres_ + ["dec%d" % (2 * hb), "dec%d" % (2 * hb + 1)], writes=["kdec"])
                for q4 in range(4):
                    bank = 2 + q4
                    pvv = ps[bank][:, :].bitcast(BF16)
                    bres_ = ["pb%d" % bank]

                    def trv(e, q4=q4, pvv=pvv):
                        for i in range(4):
                            tt = q4 * 4 + i
                            for dvh in range(2):
                                ins = e.transpose(pvv[:, i * 256 + dvh * 128:i * 256 + (dvh + 1) * 128],
                                                  gvT[dvh][:, tt * 128:(tt + 1) * 128], ident)
                        return ins
                    P.add("tensor", trv, reads=["gvT0", "gvT1", "ident"], writes=bres_)
                    eng = "scalar" if q4 % 2 else "vector"
                    if eng == "scalar":
                        P.add("scalar", lambda e, q4=q4, pvv=pvv: e.copy(out=gvtok[:, q4 * 1024:(q4 + 1) * 1024], in_=pvv),
                              reads=bres_, writes=["gvtok"])
                    else:
                        P.add("vector", lambda e, q4=q4, pvv=pvv: e.tensor_copy(out=gvtok[:, q4 * 1024:(q4 + 1) * 1024], in_=pvv),
                              reads=bres_, writes=["gvtok"])
            def gla_rec(h, l=l):
                hb = h % 2
                P.add("gpsimd", lambda e: e.memset(Sf[1], 0.0), writes=["Sf1"])

                def emit_U(c):
                    tt, hf = c // 2, c % 2
                    r0 = hf * 64
                    pu = ps[6 + c % 2][:, 0:256]
                    P.add("tensor", lambda e: e.matmul(pu, lhsT=kdec[r0:r0 + 64, tt * 128:(tt + 1) * 128],
                                                       rhs=gvtok[r0:r0 + 64, tt * 256:(tt + 1) * 256], start=True, stop=True),
                          reads=["kdec", "gvtok"], writes=["pb%d" % (6 + c % 2)])

                def emit_S(c):
                    pu = ps[6 + c % 2][:, 0:256]
                    cur, prv = c % 2, (c + 1) % 2
                    P.add("vector", lambda e: e.scalar_tensor_tensor(out=Sf[cur], in0=Sf[prv], scalar=Aexp[:, c:c + 1], op0=ALU.mult,
                                                                     in1=pu, op1=ALU.add),
                          reads=["Sf%d" % prv, "Aexp", "pb%d" % (6 + c % 2)], writes=["Sf%d" % cur])
                    P.add("scalar", lambda e: e.copy(out=Sb[cur], in_=Sf[cur]), reads=["Sf%d" % cur], writes=["Sb%d" % cur])

                def emit_O(c):
                    slab, ci = c // 8, c % 8
                    sp = slab % 2
                    cur = c % 2

                    def of(e):
                        for dvh in range(2):
                            ins = e.matmul(ps[2 + dvh][:, ci * 64:(ci + 1) * 64], lhsT=Sb[cur][:, dvh * 128:(dvh + 1) * 128],
                                           rhs=gqT[hb][:, c * 64:(c + 1) * 64], start=True, stop=True)
                        return ins
                    pres_ = ["pb%d" % (2 + dvh) for dvh in range(2)]
                    P.add("tensor", of, reads=["Sb%d" % cur, "gqT%d" % hb], writes=pres_)
                    if ci == 7:
                        for dvh in range(2):
                            bank = 2 + dvh
                            P.add("scalar", lambda e, dvh=dvh, bank=bank: e.activation(
                                out=oun[sp][:, dvh * 512:(dvh + 1) * 512], in_=ps[bank][:, :], func=AF.Copy, scale=SCALE_GLA),
                                reads=["pb%d" % bank], writes=["oun%d" % sp])
                        P.add("scalar", lambda e: e.activation(out=sqg[sp], in_=oun[sp], func=AF.Square),
                              reads=["oun%d" % sp], writes=["sqg%d" % sp])

                        def msf(e):
                            for dvh in range(2):
                                ins = e.matmul(ps[4][:, :], lhsT=ones, rhs=sqg[sp][:, dvh * 512:(dvh + 1) * 512],
                                               start=(dvh == 0), stop=(dvh == 1))
                            return ins
                        P.add("tensor", msf, reads=["sqg%d" % sp, "ones"], writes=["pb4"])
                        P.add("scalar", lambda e: e.activation(out=rsg[sp], in_=ps[4][:, :], func=AF.Ln, bias=EPS, scale=1.0 / 256.0),
                              reads=["pb4"], writes=["rsg%d" % sp])
                        P.add("scalar", lambda e: e.activation(out=rsg[sp], in_=rsg[sp], func=AF.Exp, scale=-0.5, bias=math.log(0.5)),
                              reads=["rsg%d" % sp], writes=["rsg%d" % sp])
                        P.add("vector", lambda e: e.tensor_tensor(
                            out=oun[sp].rearrange("p (d t) -> p d t", d=2), in0=oun[sp].rearrange("p (d t) -> p d t", d=2),
                            in1=rsg[sp].unsqueeze(1).to_broadcast([128, 2, 512]), op=ALU.mult),
                            reads=["oun%d" % sp, "rsg%d" % sp], writes=["oun%d" % sp])
                        for dvh in range(2):
                            P.add("vector", lambda e, dvh=dvh: e.scalar_tensor_tensor(
                                out=ofg[sp][:, dvh * 512:(dvh + 1) * 512], in0=oun[sp][:, dvh * 512:(dvh + 1) * 512],
                                scalar=gcol(l, 36 + 2 * h + dvh), op0=ALU.mult,
                                in1=ggf[hb][dvh][:, slab * 512:(slab + 1) * 512], op1=ALU.mult),
                                reads=["oun%d" % sp, "gains", "ggf%d_%d" % (hb, dvh)], writes=["ofg%d" % sp])

                        def dm(e):
                            r = []
                            for dvh in range(2):
                                r.append(e.dma_start(out=OCgm_d[h][dvh * 128:(dvh + 1) * 128, slab * 512:(slab + 1) * 512],
                                                     in_=ofg[sp][:, dvh * 512:(dvh + 1) * 512]))
                            return r
                        P.add("sync", dm, reads=["ofg%d" % sp], writes=["OCgm%d" % h], slot="ofg%d" % sp, ndma=2)

                emit_U(0)
                for c in range(32):
                    emit_S(c)
                    if c + 1 < 32:
                        emit_U(c + 1)
                    emit_O(c)
                    P.mark()

            def ht4(t0):
                return V(o_RH + t0 * 1024, 2 * 2048, BF16).rearrange("p (j t) -> p j t", j=2)

            def gla_exchange(h):
                P.add("gpsimd", lambda e: e.collective_compute("AllGather", ALU.bypass, replica_groups=GROUPS,
                                                               ins=[OCgm_d[h]], outs=[OCgf_d[h]]),
                      reads=["OCgm%d" % h], writes=["OCgf%d" % h], slot="ccg%d" % h, sinc=1)

                def ld(e):
                    r = []
                    for rk in range(2):
                        r.append(e.dma_start(out=ht4(4 * rk + 2 * h),
                                             in_=OCgf_d[h][rk * 256:(rk + 1) * 256, :].rearrange("(j p) t -> p j t", p=128)))
                    return r
                P.add("gpsimd", ld, reads=["OCgf%d" % h], writes=["hT%d.%d" % (4 * rk + 2 * h + d, s_) for rk in range(2) for d in range(2) for s_ in range(4)],
                      slot="oclg%d" % h, ndma=2)

            def att_load():
                def ld(e):
                    r = []
                    for g4 in range(2):
                        r.append(e.dma_start(out=V(o_RH + (8 + 4 * g4) * 1024, 4 * 2048, BF16).rearrange("p (j t) -> p j t", j=4),
                                             in_=OCaf_d[g4 * 512:(g4 + 1) * 512, :].rearrange("(j p) t -> p j t", p=128)))
                    return r
                P.add("gpsimd", ld, reads=["OCaf"], writes=["hT%d.%d" % (j, s_) for j in range(8, 16) for s_ in range(4)], slot="ocla", ndma=2)

            for h in range(NHG):
                gate_prep(h)
                if h + 1 < NHG:
                    rx_ = P.capture(lambda: gla_rec(h))
                    ry_ = P.capture(lambda: gla_proj(h + 1))
                    P.merge(rx_, ry_)
                else:
                    gla_rec(h)
                gla_exchange(h)
                if h == NHG - 2:
                    att_load()
            P.barrier()
            if stop == "G":
                break

            pos[0] = o_RM
            ysl = [V(A(16 * 512 * 4), 8192) for _ in range(2)]
            sqo = [[V(A(1024), 512, BF16) for _ in range(16)] for _ in range(2)]
            rso = [V(A(2048), 512) for _ in range(2)]
            NXR = 6
            xres = [V(A(2048), 512) for _ in range(NXR)]
            rsq = [V(A(2048), 512) for _ in range(2)]
            assert (pos[0] - o_RM) * 4 <= RM_BYTES, ((pos[0] - o_RM) * 4, RM_BYTES)
            if stop == "O1":
                P.barrier()
                break
            ostate = {"bank": 0, "xi": 0, "xl": 0, "defer": []}

            def o_group(s, j):
                sp = s % 2
                wj = next_block()
                bank = ostate["bank"]
                ostate["bank"] = (bank + 1) % 6

                def mm(e):
                    for kt in range(NKT):
                        ins = e.matmul(ps[bank][:, :], lhsT=WB[wj][:, kt * 128:(kt + 1) * 128],
                                       rhs=HT[kt][:, s * 512:(s + 1) * 512], start=(kt == 0), stop=(kt == NKT - 1))
                    return ins
                P.add("tensor", mm, reads=["wb%d" % wj] + ["hT%d.%d" % (kt, s) for kt in range(NKT)], writes=["pb%d" % bank])
                P.add("scalar", lambda e: e.copy(out=ysl[sp][:, j * 512:(j + 1) * 512], in_=ps[bank][:, :]),
                      reads=["pb%d" % bank], writes=["ysl%d_%d" % (sp, j)])
                P.add("scalar", lambda e: e.activation(out=sqo[sp][j], in_=ps[bank][:, :], func=AF.Square),
                      reads=["pb%d" % bank], writes=["sqo%d_%d" % (sp, j)])

            def o_stats(s):
                sp = s % 2

                def mso(e):
                    for j in range(16):
                        ins = e.matmul(ps[7][:, :], lhsT=ones, rhs=sqo[sp][j], start=(j == 0), stop=(j == 15))
                    return ins
                P.add("tensor", mso, reads=["sqo%d_%d" % (sp, j) for j in range(16)] + ["ones"], writes=["pb7"])
                P.add("scalar", lambda e: e.activation(out=rso[sp], in_=ps[7][:, :], func=AF.Ln, bias=EPS, scale=1.0 / DM),
                      reads=["pb7"], writes=["rso%d" % sp])
                P.add("scalar", lambda e: e.activation(out=rso[sp], in_=rso[sp], func=AF.Exp, scale=-0.5),
                      reads=["rso%d" % sp], writes=["rso%d" % sp])

            def o_xload(n, src_d=src_d):
                if n >= 64 or n < ostate["xl"]:
                    return
                ostate["xl"] = n + 1
                s_, j_ = n // 16, n % 16
                xb_ = n % NXR
                P.add("sync", lambda e: e.dma_start(out=xres[xb_], in_=src_d[j_ * 128:(j_ + 1) * 128, s_ * 512:(s_ + 1) * 512]),
                      reads=[XR[j_][s_]], writes=["xres%d" % xb_], slot="xres%d" % xb_)

            def o_pre(s, l=l):
                sp = s % 2

                def msp(e):
                    for j in range(16):
                        ins = e.matmul(ps[6][:, :], lhsT=ones, rhs=sqo[sp][j], start=(j == 0), stop=(j == 15))
                    return ins
                P.add("tensor", msp, reads=["sqo%d_%d" % (sp, j) for j in range(16)] + ["ones"], writes=["pb6"])
                P.add("scalar", lambda e: e.activation(out=rsq[sp], in_=ps[6][:, :], func=AF.Ln, bias=EPS, scale=1.0 / DM),
                      reads=["pb6"], writes=["rsq%d" % sp])
                P.add("scalar", lambda e: e.activation(out=rsq[sp], in_=rsq[sp], func=AF.Exp, scale=-0.5),
                      reads=["rsq%d" % sp], writes=["rsq%d" % sp])
                for j in range(16):
                    ostate["defer"].append(lambda j=j: P.add("vector", lambda e: e.scalar_tensor_tensor(
                        out=HT[j][:, s * 512:(s + 1) * 512], in0=ysl[sp][:, j * 512:(j + 1) * 512], scalar=gcol(l + 1, j),
                        op0=ALU.mult, in1=rsq[sp], op1=ALU.mult),
                        reads=["ysl%d_%d" % (sp, j), "rsq%d" % sp, "gains"], writes=["hT%d.%d" % (j, s)]))

            def o_epi(s, j, l=l, src_d=src_d, dst_d=dst_d):
                if ostate["xi"] == 0:
                    for n_ in range(NXR - 1):
                        o_xload(n_)
                sp = s % 2
                xb = ostate["xi"] % NXR
                ostate["xi"] += 1
                xr = XR[j][s]
                yv = ysl[sp][:, j * 512:(j + 1) * 512]
                yr = "ysl%d_%d" % (sp, j)
                o_xload(ostate["xi"] + NXR - 2)
                P.add("vector", lambda e: e.tensor_tensor(out=yv, in0=yv, in1=rso[sp], op=ALU.mult),
                      reads=[yr, "rso%d" % sp], writes=[yr])
                P.add("vector", lambda e: e.scalar_tensor_tensor(out=yv, in0=yv, scalar=gcol(l, 16 + j), op0=ALU.mult,
                                                                 in1=xres[xb], op1=ALU.add),
                      reads=[yr, "gains", "xres%d" % xb], writes=[yr])
                P.add("sync", lambda e: e.dma_start(out=dst_d[j * 128:(j + 1) * 128, s * 512:(s + 1) * 512], in_=yv),
                      reads=[yr], writes=[xr], slot="xst%d_%d" % (sp, j))
                if l + 1 < depth:
                    P.add("scalar", lambda e: e.activation(out=sqo[sp][j], in_=yv, func=AF.Square),
                          reads=[yr], writes=["sqo%d_%d" % (sp, j)])
                    if j == 15:
                        o_pre(s)

            for s in range(4):
                for j in range(16):
                    for _ in range(2):
                        if ostate["defer"]:
                            ostate["defer"].pop(0)()
                    o_group(s, j)
                    if s > 0 and stop != "O2":
                        if j == 0:
                            for n_ in range(NXR - 1):
                                o_xload((s - 1) * 16 + n_)
                        if j == 1:
                            o_stats(s - 1)
                        if 5 <= j <= 12:
                            o_epi(s - 1, 2 * (j - 5))
                            o_epi(s - 1, 2 * (j - 5) + 1)
            if stop != "O2":
                o_stats(3)
                for j in range(16):
                    o_epi(3, j)
            while ostate["defer"]:
                ostate["defer"].pop(0)()
            P.barrier()

        P.add("sync", None, reads=[XR[j][s] for j in range(16) for s in range(4)])
        P.finalize(st)
    return nc


def _half_cols(half):
    AQ, AK, AV, AG = 3088, 3088 + 1024, 3088 + 2048, 3088 + 3072
    blocks = [None] * NBLK
    for al in range(NHA):
        a = NHA * half + al
        for pos, base in ((0, AQ), (4, AK), (8, AV), (12, AG)):
            blocks[pos + al] = np.arange(base + a * 128, base + (a + 1) * 128)
    for hl in range(NHG):
        h = NHG * half + hl
        blocks[16 + hl] = np.arange(0 + h * 128, 0 + (h + 1) * 128)
        blocks[18 + hl] = np.arange(512 + h * 128, 512 + (h + 1) * 128)
        for d in range(2):
            blocks[20 + 2 * hl + d] = np.arange(1024 + h * 256 + d * 128, 1024 + h * 256 + (d + 1) * 128)
            blocks[24 + 2 * hl + d] = np.arange(2048 + h * 256 + d * 128, 2048 + h * 256 + (d + 1) * 128)
    return np.concatenate(blocks)


def _prep_shared(w_in, w_out, g_pre, g_post, layers):
    nl = len(layers)
    w_ga = np.empty((nl, 128, 256), np.float32)
    w_outr = np.empty((nl * 16, 128, 2048), np.float32)
    for i, l in enumerate(layers):
        Wg = np.asarray(w_in[l])[:, 3072:3088].reshape(16, 128, 16)
        w_ga[i] = Wg.transpose(1, 0, 2).reshape(128, 256)
        Wo = np.asarray(w_out[l]).reshape(16, 128, 16, 128)
        w_outr[i * 16:(i + 1) * 16] = Wo.transpose(2, 1, 0, 3).reshape(16, 128, 2048)
    return {"w_ga": w_ga, "w_outr": w_outr}


def _prep_half(w_in, g_pre, g_post, w_alpha, b_alpha, g_gla, g_att, rel_bias, layers, half):
    nl = len(layers)
    cols = _half_cols(half)
    w_main = np.empty((nl * NBLK, 128, 2048), np.float32)
    gains = np.empty((128, nl * NG), np.float32)
    wal = np.zeros((nl, 32, 256), np.float32)
    biasT = np.empty((nl * NHA, 128, 256), np.float32)
    k = np.arange(128)[:, None]
    q = np.arange(128)[None, :]
    idx0 = np.clip(q - k, -128, 128) + 128
    idx1 = np.clip(128 + q - k, -128, 128) + 128
    for i, l in enumerate(layers):
        W = np.asarray(w_in[l])
        Wm = W[:, cols].reshape(16, 128, NBLK, 128)
        w_main[i * NBLK:(i + 1) * NBLK] = Wm.transpose(2, 1, 0, 3).reshape(NBLK, 128, 2048)
        g0 = i * NG
        gains[:, g0 + 0:g0 + 16] = np.asarray(g_pre[l]).reshape(16, 128).T
        gains[:, g0 + 16:g0 + 32] = np.asarray(g_post[l]).reshape(16, 128).T
        gains[:, g0 + 32:g0 + 36] = np.asarray(g_att[l]).reshape(8, 128)[NHA * half:NHA * (half + 1)].T
        gains[:, g0 + 36:g0 + 40] = np.asarray(g_gla[l]).reshape(8, 128)[4 * half:4 * (half + 1)].T
        rb = np.asarray(rel_bias[l])
        gains[:, g0 + 40:g0 + 44] = np.broadcast_to(rb[NHA * half:NHA * (half + 1), 256][None, :], (128, NHA))
        wal[i, 0:16] = np.asarray(w_alpha[l])[:, 256 * half:256 * (half + 1)]
        wal[i, 16] = np.asarray(b_alpha[l])[256 * half:256 * (half + 1)]
        for al in range(NHA):
            h = NHA * half + al
            biasT[i * NHA + al, :, 0:128] = rb[h][idx0]
            biasT[i * NHA + al, :, 128:256] = rb[h][idx1]
    return {"w_main": w_main, "gains": gains, "wal": wal, "biasT": biasT}


_NC_CACHE = {}
NCORES = 8


def _get_nc(depth):
    if depth not in _NC_CACHE:
        _NC_CACHE[depth] = build(depth)
    return _NC_CACHE[depth]


def kernel(x, w_in, w_out, g_pre, g_post, w_alpha, b_alpha, g_gla, g_att, rel_bias):
    x = np.asarray(x, np.float32)
    B = x.shape[0]
    layers = list(range(DEPTH))
    nc = _get_nc(DEPTH)
    shared = _prep_shared(w_in, w_out, g_pre, g_post, layers)
    halves = [_prep_half(w_in, g_pre, g_post, w_alpha, b_alpha, g_gla, g_att, rel_bias, layers, hf) for hf in range(2)]
    in_maps = []
    for c in range(NCORES):
        m = dict(shared)
        m.update(halves[c % 2])
        m["xT"] = np.ascontiguousarray(x[c // 2].T)
        in_maps.append(m)
    res = run_bass_kernel_spmd(nc, in_maps, core_ids=list(range(NCORES)))
    out = np.stack([np.asarray(res.results[2 * b]["outT"], np.float32).T for b in range(B)], axis=0)
    return np.ascontiguousarray(out.astype(np.float32))
```

```python
import contextlib
import math
import os
import numpy as np
import concourse.bass as bass
import concourse.mybir as mybir
from concourse.bass_utils import run_bass_kernel_spmd

F32 = mybir.dt.float32
BF16 = mybir.dt.bfloat16
AF = mybir.ActivationFunctionType
ALU = mybir.AluOpType

EPOCH = 30000
ENGINES = ("sync", "gpsimd", "scalar", "vector", "tensor")
SAFE_SELF = ("tensor",)

DEPTH = 4
DM = 2048
SEQ = 2048
NKT = 16
EPS = 1e-6
NG = 44
NHA = 4
NHG = 2
NBLK = 28
GROUPS = [[0, 1], [2, 3], [4, 5], [6, 7]]
SCALE_ATT = 128.0 ** -0.5
SCALE_GLA = 128.0 ** -0.5
NEG = -30000.0


class Op:
    __slots__ = ("eng", "fn", "reads", "writes", "slot", "ndma", "waits", "tok", "bar", "sinc")

    def __init__(self, eng, fn, reads, writes, slot, ndma, bar=False, sinc=16):
        self.sinc = sinc
        self.eng = eng
        self.fn = fn
        self.reads = tuple(reads)
        self.writes = tuple(writes)
        self.slot = slot
        self.ndma = ndma
        self.waits = []
        self.tok = None
        self.bar = bar


class Prog:
    def __init__(self, nc):
        self.nc = nc
        self.ops = []

    def add(self, eng, fn, reads=(), writes=(), slot=None, ndma=1, sinc=16):
        self.ops.append(Op(eng, fn, reads, writes, slot, ndma, sinc=sinc))

    def mark(self):
        self.ops.append(None)

    def capture(self, fn):
        n0 = len(self.ops)
        fn()
        got = self.ops[n0:]
        del self.ops[n0:]
        return got

    def merge(self, xs, ys):
        def chunks(lst):
            out, cur = [], []
            for o in lst:
                if o is None:
                    if cur:
                        out.append(cur)
                    cur = []
                else:
                    cur.append(o)
            if cur:
                out.append(cur)
            return out
        cx, cy = chunks(xs), chunks(ys)
        for i in range(max(len(cx), len(cy))):
            if i < len(cx):
                self.ops.extend(cx[i])
            if i < len(cy):
                self.ops.extend(cy[i])

    def barrier(self):
        for e in ENGINES:
            self.ops.append(Op(e, None, (), (), None, 1, bar=True))

    def finalize(self, stack):
        nc = self.nc
        self.ops = [o for o in self.ops if o is not None]
        state = {}
        cnt = {}
        latest = {}
        known = {e: {} for e in ENGINES}
        ecount = {e: 0 for e in ENGINES}
        keys = []
        for op in self.ops:
            waits = {}

            def need(tok, op=op, waits=waits):
                if tok is None:
                    return
                k, v = tok
                if k[0] == "E" and k[1] == op.eng and op.eng in SAFE_SELF:
                    return
                if known[op.eng].get(k, 0) < v and waits.get(k, 0) < v:
                    waits[k] = v

            if op.bar:
                for k, v in latest.items():
                    if k[0] == "D" and (k[1].startswith("cc") or k[1].startswith("wb")):
                        continue
                    need((k, v))
            for r in op.reads:
                st = state.get(r)
                if st is not None:
                    need(st[0])
            for w in op.writes:
                st = state.get(w)
                if st is not None:
                    need(st[0])
                    for k, v in st[1].items():
                        need((k, v))
            for k, v in waits.items():
                known[op.eng][k] = v
            op.waits = list(waits.items())
            if op.fn is None:
                continue
            if op.slot is not None:
                key = ("D", op.slot)
                cnt[key] = cnt.get(key, 0) + op.sinc * op.ndma
                tok = (key, cnt[key])
            else:
                ecount[op.eng] += 1
                ep = (ecount[op.eng] - 1) // EPOCH
                key = ("E", op.eng, ep)
                tok = (key, ecount[op.eng] - ep * EPOCH)
            if key not in latest:
                keys.append(key)
            latest[key] = tok[1]
            op.tok = tok
            for r in op.reads:
                st = state.setdefault(r, [None, {}])
                if st[1].get(tok[0], 0) < tok[1]:
                    st[1][tok[0]] = tok[1]
            for w in op.writes:
                state[w] = [tok, {}]
        sems = {}
        for i, key in enumerate(keys):
            sems[key] = stack.enter_context(nc.semaphore("s%d" % i))
        self.nsems = len(keys)
        block = stack.enter_context(nc.Block())
        for ename in ENGINES:
            eops = [op for op in self.ops if op.eng == ename]
            if not eops:
                continue

            def body(e, eops=eops):
                for op in eops:
                    for k, v in op.waits:
                        e.wait_ge(sems[k], v)
                    if op.fn is None:
                        continue
                    ins = op.fn(e)
                    if op.slot is not None:
                        if not isinstance(ins, (list, tuple)):
                            ins = [ins]
                        assert len(ins) == op.ndma, (len(ins), op.ndma)
                        for i_ in ins:
                            i_.then_inc(sems[op.tok[0]], op.sinc)
                    else:
                        ins.then_inc(sems[op.tok[0]], 1)

            getattr(block, ename)(body)


def build(depth, stop=None):
    nc = bass.Bass("TRN2", target_bir_lowering=False)
    xT_d = nc.dram_tensor("xT", [DM, SEQ], F32, kind="ExternalInput").ap()
    wmain_d = nc.dram_tensor("w_main", [depth * NBLK, 128, 2048], F32, kind="ExternalInput").ap()
    wga_d = nc.dram_tensor("w_ga", [depth, 128, 256], F32, kind="ExternalInput").ap()
    wout_d = nc.dram_tensor("w_outr", [depth * 16, 128, 2048], F32, kind="ExternalInput").ap()
    gains_d = nc.dram_tensor("gains", [128, depth * NG], F32, kind="ExternalInput").ap()
    wal_d = nc.dram_tensor("wal", [depth, 32, 256], F32, kind="ExternalInput").ap()
    biasT_d = nc.dram_tensor("biasT", [depth * NHA, 128, 256], F32, kind="ExternalInput").ap()
    outT_d = nc.dram_tensor("outT", [DM, SEQ], F32, kind="ExternalOutput").ap()
    XS_d = nc.dram_tensor("xs_scratch", [DM, SEQ], F32).ap() if depth > 1 else None
    OCam_d = nc.dram_tensor("oca_mine", [512, SEQ], BF16).ap()
    OCgm_d = [nc.dram_tensor("ocg_mine%d" % i, [256, SEQ], BF16).ap() for i in range(NHG)]
    OCaf_d = nc.dram_tensor("oca_full", [1024, SEQ], BF16).ap()
    OCgf_d = [nc.dram_tensor("ocg_full%d" % i, [512, SEQ], BF16).ap() for i in range(NHG)]

    with contextlib.ExitStack() as st:
        NW = 52480
        arena = st.enter_context(nc.sbuf_tensor("arena", [128, NW], F32))
        ps = [st.enter_context(nc.psum_tensor("ps%d" % i, [128, 512], F32)) for i in range(8)]
        P = Prog(nc)

        def V(off, n, dt=F32, p0=0, p1=128):
            if dt == F32:
                return arena[p0:p1, off:off + n]
            assert n % 2 == 0
            return arena[p0:p1, off:off + n // 2].bitcast(BF16)

        pos = [0]

        def A(nbytes):
            o = pos[0]
            pos[0] += (nbytes + 3) // 4
            return o

        o_RH = A(16 * 2048 * 2)
        HT = [V(o_RH + j * 1024, 2048, BF16) for j in range(16)]
        NWB = 4
        o_WB = [A(4096) for _ in range(NWB)]
        WB = [V(o, 2048, BF16) for o in o_WB]
        ones = V(A(256), 128, BF16)
        ident = V(A(256), 128, BF16)
        triU = V(A(512), 128)
        cind = V(A(16), 2)
        gains = V(A(depth * NG * 4), depth * NG)
        o_wal = A(1024)
        wal = V(o_wal, 256, F32, 0, 32)
        biasb = [V(A(1024), 256) for _ in range(2)]
        wgas = V(A(1024), 256)
        wgab = V(A(512), 256, BF16)
        tnh = V(A(2048), 512)
        identf = tnh[:, 0:128]
        o_RM = pos[0]
        RM_BYTES = (NW - o_RM) * 4
        E_WORDS = 6 * 1024
        o_E = NW - E_WORDS
        gqT = [V(o_E, 2048, BF16), None]
        gkT = V(o_E + 1024, 2048, BF16)
        gvT = [V(o_E + 2048, 2048, BF16), V(o_E + 3072, 2048, BF16)]
        ggf = [[V(o_E + 4096, 2048, BF16), V(o_E + 5120, 2048, BF16)], [None, None]]

        def gla_proj(h):
            hb = h % 2
            project(ev_copy("scalar", gqT[hb], "gqT%d" % hb), mark=True)
            project(ev_copy("vector", gkT, "gkT"), mark=True)
            project(ev_copy("vector", gvT[0], "gvT0"), mark=True)
            project(ev_copy("scalar", gvT[1], "gvT1"), mark=True)
            project(ev_silu(ggf[hb][0], "ggf%d_0" % hb), mark=True)
            project(ev_silu(ggf[hb][1], "ggf%d_1" % hb), mark=True)

        def gcol(l, k):
            c = l * NG + k
            return gains[:, c:c + 1]

        sched = []
        for l in range(depth):
            for a in range(NHA):
                for t in (0, 4, 8, 12):
                    sched.append(wmain_d[l * NBLK + t + a])
            for h in range(NHG):
                for t in (16 + h, 18 + h, 20 + 2 * h, 21 + 2 * h, 24 + 2 * h, 25 + 2 * h):
                    sched.append(wmain_d[l * NBLK + t])
            for s in range(4):
                for j in range(16):
                    sched.append(wout_d[l * 16 + j])
        wstate = {"loaded": 0, "used": 0}

        def emit_load(n):
            src = sched[n]
            j = n % NWB
            P.add("gpsimd", lambda e, j=j, src=src: e.dma_start(out=WB[j], in_=src), writes=["wb%d" % j], slot="wb%d" % j)

        def next_block():
            n = wstate["used"]
            wstate["used"] += 1
            while wstate["loaded"] < min(n + NWB, len(sched)):
                emit_load(wstate["loaded"])
                wstate["loaded"] += 1
            return n % NWB

        acc = [0]
        HTR = ["hT%d.%d" % (j, s_) for j in range(16) for s_ in range(4)]

        def project(evac, src_res=HTR, M=128, wb=None, wres=None, mark=False):
            if wb is None:
                j = next_block()
                wbv = WB[j]
                wres = "wb%d" % j
                stride = 128
            else:
                wbv = wb
                stride = M
            for s in range(4):
                bank = acc[0]
                acc[0] ^= 1

                def mm(e, s=s, bank=bank, wbv=wbv, stride=stride, k0=0, k1=NKT):
                    for kt in range(k0, k1):
                        ins = e.matmul(ps[bank][0:M, :], lhsT=wbv[:, kt * stride:kt * stride + M],
                                       rhs=HT[kt][:, s * 512:(s + 1) * 512], start=(kt == 0), stop=(kt == NKT - 1))
                    return ins

                if mark:
                    P.add("tensor", lambda e, mm=mm: mm(e, k0=0, k1=NKT // 2), reads=[wres] + list(src_res), writes=["pb%d" % bank])
                    P.mark()
                    P.add("tensor", lambda e, mm=mm: mm(e, k0=NKT // 2, k1=NKT), reads=[wres] + list(src_res), writes=["pb%d" % bank])
                else:
                    P.add("tensor", mm, reads=[wres] + list(src_res), writes=["pb%d" % bank])
                evac(s, ps[bank], "pb%d" % bank)
                if mark:
                    P.mark()

        def ev_copy(eng, dst, dres):
            def f(s, pb, pres):
                if eng == "scalar":
                    P.add("scalar", lambda e, s=s, pb=pb: e.copy(out=dst[:, s * 512:(s + 1) * 512], in_=pb[:, :]),
                          reads=[pres], writes=[dres])
                else:
                    P.add("vector", lambda e, s=s, pb=pb: e.tensor_copy(out=dst[:, s * 512:(s + 1) * 512], in_=pb[:, :]),
                          reads=[pres], writes=[dres])
            return f

        def ev_silu(dst, dres):
            def f(s, pb, pres):
                P.add("scalar", lambda e, pb=pb: e.activation(out=tnh, in_=pb[:, :], func=AF.Tanh, scale=0.5),
                      reads=[pres], writes=["tnh"])
                P.add("vector", lambda e, s=s, pb=pb: e.scalar_tensor_tensor(out=dst[:, s * 512:(s + 1) * 512], in0=tnh, scalar=1.0,
                                                                            op0=ALU.add, in1=pb[:, :], op1=ALU.mult),
                      reads=[pres, "tnh"], writes=[dres])
            return f

        P.add("sync", lambda e: e.dma_start(out=gains, in_=gains_d), writes=["gains"], slot="gains")
        P.add("gpsimd", lambda e: e.memset(ones, 1.0), writes=["ones"])
        P.add("gpsimd", lambda e: e.memset(identf, 1.0), writes=["identf"])
        P.add("gpsimd", lambda e: e.affine_select(out=identf, in_=identf, pattern=[[1, 128]], compare_op=ALU.is_equal,
                                                  fill=0.0, base=0, channel_multiplier=-1), reads=["identf"], writes=["identf"])
        P.add("gpsimd", lambda e: e.tensor_copy(out=ident, in_=identf), reads=["identf"], writes=["ident"])
        P.add("gpsimd", lambda e: e.memset(triU, 1.0), writes=["triU"])
        P.add("gpsimd", lambda e: e.affine_select(out=triU, in_=triU, pattern=[[-1, 128]], compare_op=ALU.is_gt,
                                                  fill=0.0, base=0, channel_multiplier=1), reads=["triU"], writes=["triU"])
        P.add("gpsimd", lambda e: e.memset(triU[64:128, 0:64], 0.0), reads=["triU"], writes=["triU"])
        P.add("gpsimd", lambda e: e.memset(cind, 0.0), writes=["cind"])
        P.add("gpsimd", lambda e: e.memset(cind[0:64, 0:1], 1.0), reads=["cind"], writes=["cind"])
        P.add("gpsimd", lambda e: e.memset(cind[64:128, 1:2], 1.0), reads=["cind"], writes=["cind"])

        XR = [["X%d.%d" % (j, s) for s in range(4)] for j in range(16)]

        def xview(dram, s0, n):
            return dram.rearrange("(j p) t -> p j t", p=128)[:, :, s0:s0 + n]

        for l in range(depth):
            src_d = xT_d if l == 0 else XS_d
            dst_d = outT_d if l == depth - 1 else XS_d

            pos[0] = o_RM
            NXS = 4
            xs = [V(A(16 * 256 * 4), 4096) for _ in range(NXS)]
            sqp = [V(A(16 * 256 * 2), 4096, BF16) for _ in range(NXS)]
            rsp = [V(A(1024), 256) for _ in range(NXS)]
            assert (pos[0] - o_RM) * 4 <= RM_BYTES
            for s in (range(8) if l == 0 else []):
                b = s % NXS
                xr = [XR[j][s // 2] for j in range(16)]
                P.add("sync", lambda e, b=b, s=s, src_d=src_d: e.dma_start(
                    out=xs[b].rearrange("p (j t) -> p j t", j=16), in_=xview(src_d, s * 256, 256)),
                    reads=xr, writes=["xs%d" % b], slot="xs%d" % b)
                P.add("scalar", lambda e, b=b: e.activation(out=sqp[b], in_=xs[b], func=AF.Square),
                      reads=["xs%d" % b], writes=["sqp%d" % b])
                half = "pb%d" % (4 + b)
                msap = ps[4 + b][:, 0:256]

                def mmp(e, b=b, msap=msap):
                    for j in range(16):
                        ins = e.matmul(msap, lhsT=ones, rhs=sqp[b][:, j * 256:(j + 1) * 256], start=(j == 0), stop=(j == 15))
                    return ins
                P.add("tensor", mmp, reads=["sqp%d" % b, "ones"], writes=[half])
                P.add("scalar", lambda e, b=b, msap=msap: e.activation(out=rsp[b], in_=msap, func=AF.Ln, bias=EPS, scale=1.0 / DM),
                      reads=[half], writes=["rsp%d" % b])
                P.add("scalar", lambda e, b=b: e.activation(out=rsp[b], in_=rsp[b], func=AF.Exp, scale=-0.5),
                      reads=["rsp%d" % b], writes=["rsp%d" % b])
                for j in range(16):
                    P.add("vector", lambda e, b=b, j=j, s=s, l=l: e.scalar_tensor_tensor(
                        out=HT[j][:, s * 256:(s + 1) * 256], in0=xs[b][:, j * 256:(j + 1) * 256], scalar=gcol(l, j),
                        op0=ALU.mult, in1=rsp[b], op1=ALU.mult),
                        reads=["xs%d" % b, "rsp%d" % b, "gains"], writes=["hT%d.%d" % (j, s // 2)])
            P.barrier()
            if stop == "P":
                break

            pos[0] = o_RM
            qT = [V(A(4096), 2048, BF16) for _ in range(2)]
            kT = [V(A(4096), 2048, BF16) for _ in range(2)]
            vT = V(A(4096), 2048, BF16)
            vtok = [V(A(4096), 2048, BF16) for _ in range(2)]
            gatef = [V(A(4096), 2048, BF16) for _ in range(3)]
            od = [V(A(16384), 4096) for _ in range(2)]
            PT = [V(A(1280), 640, BF16) for _ in range(2)]
            tmpn = [V(A(1024), 256) for _ in range(2)]
            sqa = V(A(4096), 2048, BF16)
            t1 = V(A(8192), 2048)
            ofin = V(A(4096), 2048, BF16)
            assert pos[0] <= o_E, (pos[0], o_E)
            for par in range(2):
                P.add("gpsimd", lambda e, par=par: e.memset(PT[par][0:64, 576:640], 0.0), writes=["PTz%d" % par])

            def att_norm(a, l=l):
                ob = a % 2
                odv = od[ob].rearrange("p (m c t) -> p m c t", m=16, c=2)
                ov = odv[:, :, 0, :]
                dv = odv[:, :, 1, :]
                odr = "od%d" % ob
                P.add("scalar", lambda e: e.activation(out=sqa.rearrange("p (m t) -> p m t", m=16), in_=ov, func=AF.Square),
                      reads=[odr], writes=["sqa"])
                P.add("scalar", lambda e: e.activation(out=t1.rearrange("p (m t) -> p m t", m=16), in_=dv, func=AF.Square,
                                                       scale=math.sqrt(EPS)), reads=[odr], writes=["t1"])
                P.mark()
                for s in range(4):
                    bank = acc[0]
                    acc[0] ^= 1
                    P.add("tensor", lambda e, s=s, bank=bank: e.matmul(ps[bank][:, :], lhsT=ones, rhs=sqa[:, s * 512:(s + 1) * 512],
                                                                       start=True, stop=True),
                          reads=["sqa", "ones"], writes=["pb%d" % bank])
                    P.add("vector", lambda e, s=s, bank=bank: e.scalar_tensor_tensor(
                        out=t1[:, s * 512:(s + 1) * 512], in0=ps[bank][:, :], scalar=1.0 / 128.0, op0=ALU.mult,
                        in1=t1[:, s * 512:(s + 1) * 512], op1=ALU.add),
                        reads=["pb%d" % bank, "t1"], writes=["t1"])
                    P.mark()
                P.add("scalar", lambda e: e.activation(out=t1, in_=t1, func=AF.Ln), reads=["t1"], writes=["t1"])
                P.add("scalar", lambda e: e.activation(out=t1, in_=t1, func=AF.Exp, scale=-0.5, bias=math.log(0.5)),
                      reads=["t1"], writes=["t1"])
                P.mark()
                P.add("vector", lambda e: e.tensor_tensor(out=ov, in0=ov, in1=t1.rearrange("p (m t) -> p m t", m=16), op=ALU.mult),
                      reads=[odr, "t1"], writes=[odr])
                P.add("vector", lambda e, a=a: e.scalar_tensor_tensor(
                    out=ofin.rearrange("p (m t) -> p m t", m=16), in0=ov, scalar=gcol(l, 32 + a), op0=ALU.mult,
                    in1=gatef[a % 3].rearrange("p (m t) -> p m t", m=16), op1=ALU.mult),
                    reads=[odr, "gains", "gatef%d" % (a % 3)], writes=["ofin"])
                P.add("sync", lambda e, a=a: e.dma_start(out=OCam_d[a * 128:(a + 1) * 128, :], in_=ofin),
                      reads=["ofin"], writes=["OCam"], slot="ofin")
                P.mark()

            def att_proj(a, l=l):
                ob = a % 2
                bb = biasb[ob]
                bres = "biasb%d" % ob
                P.add("sync", lambda e: e.dma_start(out=bb, in_=biasT_d[l * NHA + a]), writes=[bres], slot=bres)
                P.add("gpsimd", lambda e: e.memset(bb[64:128, 0:64], NEG), reads=[bres], writes=[bres])
                project(ev_copy("vector", qT[ob], "qT%d" % ob), mark=True)
                project(ev_copy("vector", kT[ob], "kT%d" % ob), mark=True)
                project(ev_copy("vector", vT, "vT"), mark=True)
                project(ev_silu(gatef[a % 3], "gatef%d" % (a % 3)), mark=True)
                for hb in range(2):
                    bank = acc[0]
                    acc[0] ^= 1
                    pv = ps[bank][:, :].bitcast(BF16)

                    def tr(e, hb=hb, pv=pv):
                        for i in range(8):
                            tt = hb * 8 + i
                            ins = e.transpose(pv[:, i * 128:(i + 1) * 128], vT[:, tt * 128:(tt + 1) * 128], ident)
                        return ins
                    P.add("tensor", tr, reads=["vT", "ident"], writes=["pb%d" % bank])
                    P.add("vector", lambda e, hb=hb, pv=pv, bank=bank: e.tensor_copy(out=vtok[ob][:, hb * 1024:(hb + 1) * 1024], in_=pv),
                          reads=["pb%d" % bank], writes=["vtok%d" % ob])
                    P.mark()

            def att_core(a, l=l):
                ob = a % 2
                bb = biasb[ob]
                bres = "biasb%d" % ob
                bfar = gcol(l, 40 + a)
                qTa, kTa, vtoka = qT[ob], kT[ob], vtok[ob]
                qr, kr, vr = "qT%d" % ob, "kT%d" % ob, "vtok%d" % ob

                def slots(m):
                    return [(m - i, i) for i in range(5) if m - i >= 0]

                def emit_scores(m):
                    par = m % 2
                    bA, bB = 2 + 2 * par, 3 + 2 * par

                    def sc(e):
                        for kt, sl in slots(m):
                            dst = ps[bA][:, sl * 128:(sl + 1) * 128] if sl < 4 else ps[bB][:, 0:128]
                            ins = e.matmul(dst, lhsT=kTa[:, kt * 128:(kt + 1) * 128], rhs=qTa[:, m * 128:(m + 1) * 128],
                                           start=True, stop=True)
                        return ins
                    P.add("tensor", sc, reads=[kr, qr], writes=["pb%d" % bA, "pb%d" % bB])

                def emit_soft(m):
                    par = m % 2
                    bA, bB = 2 + 2 * par, 3 + 2 * par
                    nc_ = 256 if m >= 1 else 128
                    P.add("vector", lambda e: e.scalar_tensor_tensor(
                        out=tmpn[par][:, 0:nc_], in0=ps[bA][:, 0:nc_], scalar=SCALE_ATT, op0=ALU.mult,
                        in1=bb[:, 0:nc_], op1=ALU.add), reads=["pb%d" % bA, bres], writes=["tmpn%d" % par])
                    P.add("scalar", lambda e: e.activation(out=PT[par][:, 0:nc_], in_=tmpn[par][:, 0:nc_], func=AF.Exp),
                          reads=["tmpn%d" % par], writes=["PT%dn" % par])
                    if m >= 2:
                        w = 256 if m >= 3 else 128
                        P.add("scalar", lambda e: e.activation(out=PT[par][:, 256:256 + w], in_=ps[bA][:, 256:256 + w], func=AF.Exp,
                                                               bias=bfar, scale=SCALE_ATT),
                              reads=["pb%d" % bA, "gains"], writes=["PT%df" % par])
                    if m >= 4:
                        P.add("scalar", lambda e: e.activation(out=PT[par][0:64, 512:576], in_=ps[bB][0:64, 0:64], func=AF.Exp,
                                                               bias=bfar[0:64, :], scale=SCALE_ATT),
                              reads=["pb%d" % bB, "gains"], writes=["PT%dc" % par])
                        P.add("scalar", lambda e: e.activation(out=PT[par][64:128, 512:640], in_=ps[bB][64:128, 0:128], func=AF.Exp,
                                                               bias=bfar[64:128, :], scale=SCALE_ATT),
                              reads=["pb%d" % bB, "gains"], writes=["PT%dd" % par])

                def emit_pv(m):
                    par = m % 2
                    half = "pb%d" % (6 + par)
                    oacc = ps[6 + par][:, 0:128]
                    dacc = ps[6 + par][:, 128:256]

                    def pvf(e):
                        sl = slots(m)
                        for n, (kt, s_) in enumerate(sl):
                            e.matmul(oacc, lhsT=vtoka[:, kt * 128:(kt + 1) * 128], rhs=PT[par][:, s_ * 128:(s_ + 1) * 128],
                                     start=(n == 0), stop=(n == len(sl) - 1))
                        for n, (kt, s_) in enumerate(sl):
                            ins = e.matmul(dacc, lhsT=ones, rhs=PT[par][:, s_ * 128:(s_ + 1) * 128],
                                           start=(n == 0), stop=(n == len(sl) - 1))
                        return ins
                    P.add("tensor", pvf, reads=["PT%dn" % par, "PT%df" % par, "PT%dc" % par, "PT%dd" % par, "PTz%d" % par, vr, "ones"], writes=[half])
                    P.add("vector", lambda e: e.tensor_copy(out=od[ob][:, m * 256:(m + 1) * 256], in_=ps[6 + par][:, 0:256]),
                          reads=[half], writes=["od%d" % ob])

                emit_scores(0)
                for m in range(16):
                    emit_soft(m)
                    if m + 1 < 16:
                        emit_scores(m + 1)
                    P.mark()
                    emit_pv(m)
                    P.mark()

            att_proj(0)
            for a in range(NHA):
                xs_ = P.capture(lambda: att_core(a))

                def ystream(a=a):
                    if a + 1 < NHA:
                        att_proj(a + 1)
                    else:
                        gla_proj(0)
                    if a >= 1:
                        att_norm(a - 1)
                ys_ = P.capture(ystream)
                P.merge(xs_, ys_)
            att_norm(NHA - 1)
            P.add("gpsimd", lambda e: e.collective_compute("AllGather", ALU.bypass, replica_groups=GROUPS,
                                                           ins=[OCam_d], outs=[OCaf_d]),
                  reads=["OCam"], writes=["OCaf"], slot="cca", sinc=1)
            P.add("sync", lambda e, l=l: e.dma_start(out=wgas, in_=wga_d[l]), writes=["wgas"], slot="wgas")
            P.add("vector", lambda e: e.tensor_copy(out=wgab, in_=wgas), reads=["wgas"], writes=["wgab"])
            P.add("gpsimd", lambda e: e.memset(wal, 0.0), writes=["wal"])
            P.add("sync", lambda e, l=l: e.dma_start(out=wal, in_=wal_d[l]), writes=["wal"], slot="wal")
            P.barrier()
            if stop == "A":
                break

            pos[0] = o_RM
            gqT[1] = V(A(4096), 2048, BF16)
            ggf[1] = [V(A(4096), 2048, BF16) for _ in range(2)]
            o_ga = A(8192)
            gaT = V(o_ga, 2048, F32, 0, 32)
            nla = V(A(8192), 2048)
            dec = V(A(8192), 2048)
            kdec = V(A(4096), 2048, BF16)
            gvtok = V(A(8192), 4096, BF16)
            Sf = [V(A(1024), 256) for _ in range(2)]
            Sb = [V(A(512), 256, BF16) for _ in range(2)]
            Aexp = V(A(128), 32)
            oun = [V(A(4096), 1024) for _ in range(2)]
            sqg = [V(A(2048), 1024, BF16) for _ in range(2)]
            rsg = [V(A(2048), 512) for _ in range(2)]
            ofg = [V(A(2048), 1024, BF16) for _ in range(2)]
            assert pos[0] <= o_E, (pos[0], o_E)
            P.add("gpsimd", lambda e: e.memset(gaT, 1.0), writes=["gaT"])

            def ev_ga(s, pb, pres):
                P.add("scalar", lambda e, s=s, pb=pb: e.copy(out=gaT[0:16, s * 512:(s + 1) * 512], in_=pb[0:16, :]),
                      reads=[pres], writes=["gaT"])
            project(ev_ga, M=16, wb=wgab, wres="wgab")

            def gate_prep(h):
                for q4 in range(4):
                    bank = 2 + q4
                    bres_ = ["pb%d" % bank]

                    def pre(e, q4=q4, bank=bank):
                        for i in range(4):
                            tt = q4 * 4 + i
                            ins = e.matmul(ps[bank][:, i * 128:(i + 1) * 128], lhsT=gaT[:, tt * 128:(tt + 1) * 128],
                                           rhs=wal[:, h * 128:(h + 1) * 128], start=True, stop=True)
                        return ins
                    P.add("tensor", pre, reads=["gaT", "wal"], writes=bres_)
                    sl = slice(q4 * 512, (q4 + 1) * 512)
                    P.add("scalar", lambda e, bank=bank, sl=sl: e.activation(out=nla[:, sl], in_=ps[bank][:, :], func=AF.Exp, scale=-1.0),
                          reads=bres_, writes=["nla%d" % q4])
                    P.add("scalar", lambda e, sl=sl: e.activation(out=nla[:, sl], in_=nla[:, sl], func=AF.Ln, bias=1.0, scale=1.0),
                          reads=["nla%d" % q4], writes=["nla%d" % q4])

                    def dmm(e, q4=q4, bank=bank):
                        for i in range(4):
                            tt = q4 * 4 + i
                            ins = e.matmul(ps[bank][:, i * 128:(i + 1) * 128], lhsT=triU, rhs=nla[:, tt * 128:(tt + 1) * 128],
                                           start=True, stop=True)
                        return ins
                    P.add("tensor", dmm, reads=["nla%d" % q4, "triU"], writes=bres_)
                    P.add("scalar", lambda e, bank=bank, sl=sl: e.activation(out=dec[:, sl], in_=ps[bank][:, :], func=AF.Exp, scale=-1.0 / 16.0),
                          reads=bres_, writes=["dec%d" % q4])

                def lend(e):
                    for tt in range(16):
                        ins = e.matmul(ps[7][:, tt * 2:tt * 2 + 2], lhsT=nla[:, tt * 128:(tt + 1) * 128], rhs=cind, start=True, stop=True)
                    return ins
                P.add("tensor", lend, reads=["nla0", "nla1", "nla2", "nla3", "cind"], writes=["pb7"])
                P.add("scalar", lambda e: e.activation(out=Aexp, in_=ps[7][:, 0:32], func=AF.Exp, scale=-1.0 / 16.0),
                      reads=["pb7"], writes=["Aexp"])
                for hb in range(2):
                    bank = 6 + hb
                    pk = ps[bank][:, :].bitcast(BF16)
                    bres_ = ["pb%d" % bank]

                    def trk(e, hb=hb, pk=pk):
                        for i in range(8):
                            tt = hb * 8 + i
                            ins = e.transpose(pk[:, i * 128:(i + 1) * 128], gkT[:, tt * 128:(tt + 1) * 128], ident)
                        return ins
                    P.add("tensor", trk, reads=["gkT", "ident"], writes=bres_)
                    P.add("vector", lambda e, hb=hb, pk=pk: e.tensor_tensor(
                        out=kdec[:, hb * 1024:(hb + 1) * 1024], in0=pk, in1=dec[:, hb * 1024:(hb + 1) * 1024], op=ALU.mult),
                        reads=bres_ + ["dec%d" % (2 * hb), "dec%d" % (2 * hb + 1)], writes=["kdec"])
                for q4 in range(4):
                    bank = 2 + q4
                    pvv = ps[bank][:, :].bitcast(BF16)
                    bres_ = ["pb%d" % bank]

                    def trv(e, q4=q4, pvv=pvv):
                        for i in range(4):
                            tt = q4 * 4 + i
                            for dvh in range(2):
                                ins = e.transpose(pvv[:, i * 256 + dvh * 128:i * 256 + (dvh + 1) * 128],
                                                  gvT[dvh][:, tt * 128:(tt + 1) * 128], ident)
                        return ins
                    P.add("tensor", trv, reads=["gvT0", "gvT1", "ident"], writes=bres_)
                    eng = "scalar" if q4 % 2 else "vector"
                    if eng == "scalar":
                        P.add("scalar", lambda e, q4=q4, pvv=pvv: e.copy(out=gvtok[:, q4 * 1024:(q4 + 1) * 1024], in_=pvv),
                              reads=bres_, writes=["gvtok"])
                    else:
                        P.add("vector", lambda e, q4=q4, pvv=pvv: e.tensor_copy(out=gvtok[:, q4 * 1024:(q4 + 1) * 1024], in_=pvv),
                              reads=bres_, writes=["gvtok"])
            def gla_rec(h, l=l):
                hb = h % 2
                P.add("gpsimd", lambda e: e.memset(Sf[1], 0.0), writes=["Sf1"])

                ub = [6, 7, 5, 0, 1] if h == NHG - 1 else [6, 7]
                nub = len(ub)

                def emit_U(c):
                    tt, hf = c // 2, c % 2
                    r0 = hf * 64
                    pu = ps[ub[c % nub]][:, 0:256]
                    P.add("tensor", lambda e: e.matmul(pu, lhsT=kdec[r0:r0 + 64, tt * 128:(tt + 1) * 128],
                                                       rhs=gvtok[r0:r0 + 64, tt * 256:(tt + 1) * 256], start=True, stop=True),
                          reads=["kdec", "gvtok"], writes=["pb%d" % ub[c % nub]])

                def emit_S(c):
                    pu = ps[ub[c % nub]][:, 0:256]
                    cur, prv = c % 2, (c + 1) % 2
                    P.add("vector", lambda e: e.scalar_tensor_tensor(out=Sf[cur], in0=Sf[prv], scalar=Aexp[:, c:c + 1], op0=ALU.mult,
                                                                     in1=pu, op1=ALU.add),
                          reads=["Sf%d" % prv, "Aexp", "pb%d" % ub[c % nub]], writes=["Sf%d" % cur])
                    P.add("scalar", lambda e: e.copy(out=Sb[cur], in_=Sf[cur]), reads=["Sf%d" % cur], writes=["Sb%d" % cur])

                def emit_O(c):
                    slab, ci = c // 8, c % 8
                    sp = slab % 2
                    cur = c % 2

                    def of(e):
                        for dvh in range(2):
                            ins = e.matmul(ps[2 + dvh][:, ci * 64:(ci + 1) * 64], lhsT=Sb[cur][:, dvh * 128:(dvh + 1) * 128],
                                           rhs=gqT[hb][:, c * 64:(c + 1) * 64], start=True, stop=True)
                        return ins
                    pres_ = ["pb%d" % (2 + dvh) for dvh in range(2)]
                    P.add("tensor", of, reads=["Sb%d" % cur, "gqT%d" % hb], writes=pres_)
                    if ci == 7:
                        for dvh in range(2):
                            bank = 2 + dvh
                            P.add("scalar", lambda e, dvh=dvh, bank=bank: e.activation(
                                out=oun[sp][:, dvh * 512:(dvh + 1) * 512], in_=ps[bank][:, :], func=AF.Copy, scale=SCALE_GLA),
                                reads=["pb%d" % bank], writes=["oun%d" % sp])
                        P.add("scalar", lambda e: e.activation(out=sqg[sp], in_=oun[sp], func=AF.Square),
                              reads=["oun%d" % sp], writes=["sqg%d" % sp])

                        def msf(e):
                            for dvh in range(2):
                                ins = e.matmul(ps[4][:, :], lhsT=ones, rhs=sqg[sp][:, dvh * 512:(dvh + 1) * 512],
                                               start=(dvh == 0), stop=(dvh == 1))
                            return ins
                        P.add("tensor", msf, reads=["sqg%d" % sp, "ones"], writes=["pb4"])
                        P.add("scalar", lambda e: e.activation(out=rsg[sp], in_=ps[4][:, :], func=AF.Ln, bias=EPS, scale=1.0 / 256.0),
                              reads=["pb4"], writes=["rsg%d" % sp])
                        P.add("scalar", lambda e: e.activation(out=rsg[sp], in_=rsg[sp], func=AF.Exp, scale=-0.5, bias=math.log(0.5)),
                              reads=["rsg%d" % sp], writes=["rsg%d" % sp])
                        P.add("vector", lambda e: e.tensor_tensor(
                            out=oun[sp].rearrange("p (d t) -> p d t", d=2), in0=oun[sp].rearrange("p (d t) -> p d t", d=2),
                            in1=rsg[sp].unsqueeze(1).to_broadcast([128, 2, 512]), op=ALU.mult),
                            reads=["oun%d" % sp, "rsg%d" % sp], writes=["oun%d" % sp])
                        for dvh in range(2):
                            P.add("vector", lambda e, dvh=dvh: e.scalar_tensor_tensor(
                                out=ofg[sp][:, dvh * 512:(dvh + 1) * 512], in0=oun[sp][:, dvh * 512:(dvh + 1) * 512],
                                scalar=gcol(l, 36 + 2 * h + dvh), op0=ALU.mult,
                                in1=ggf[hb][dvh][:, slab * 512:(slab + 1) * 512], op1=ALU.mult),
                                reads=["oun%d" % sp, "gains", "ggf%d_%d" % (hb, dvh)], writes=["ofg%d" % sp])

                        def dm(e):
                            r = []
                            for dvh in range(2):
                                r.append(e.dma_start(out=OCgm_d[h][dvh * 128:(dvh + 1) * 128, slab * 512:(slab + 1) * 512],
                                                     in_=ofg[sp][:, dvh * 512:(dvh + 1) * 512]))
                            return r
                        P.add("sync", dm, reads=["ofg%d" % sp], writes=["OCgm%d" % h], slot="ofg%d" % sp, ndma=2)

                for c0 in range(nub - 1):
                    emit_U(c0)
                for c in range(32):
                    emit_S(c)
                    if c + nub - 1 < 32:
                        emit_U(c + nub - 1)
                    emit_O(c)
                    P.mark()

            def ht4(t0):
                return V(o_RH + t0 * 1024, 2 * 2048, BF16).rearrange("p (j t) -> p j t", j=2)

            def gla_exchange(h):
                P.add("gpsimd", lambda e: e.collective_compute("AllGather", ALU.bypass, replica_groups=GROUPS,
                                                               ins=[OCgm_d[h]], outs=[OCgf_d[h]]),
                      reads=["OCgm%d" % h], writes=["OCgf%d" % h], slot="ccg%d" % h, sinc=1)

                def ld(e):
                    r = []
                    for rk in range(2):
                        r.append(e.dma_start(out=ht4(4 * rk + 2 * h),
                                             in_=OCgf_d[h][rk * 256:(rk + 1) * 256, :].rearrange("(j p) t -> p j t", p=128)))
                    return r
                P.add("gpsimd", ld, reads=["OCgf%d" % h], writes=["hT%d.%d" % (4 * rk + 2 * h + d, s_) for rk in range(2) for d in range(2) for s_ in range(4)],
                      slot="oclg%d" % h, ndma=2)

            def att_load():
                def ld(e):
                    r = []
                    for g4 in range(2):
                        r.append(e.dma_start(out=V(o_RH + (8 + 4 * g4) * 1024, 4 * 2048, BF16).rearrange("p (j t) -> p j t", j=4),
                                             in_=OCaf_d[g4 * 512:(g4 + 1) * 512, :].rearrange("(j p) t -> p j t", p=128)))
                    return r
                P.add("gpsimd", ld, reads=["OCaf"], writes=["hT%d.%d" % (j, s_) for j in range(8, 16) for s_ in range(4)], slot="ocla", ndma=2)

            for h in range(NHG):
                gate_prep(h)
                if h + 1 < NHG:
                    rx_ = P.capture(lambda: gla_rec(h))
                    ry_ = P.capture(lambda: gla_proj(h + 1))
                    P.merge(rx_, ry_)
                else:
                    gla_rec(h)
                gla_exchange(h)
                if h == NHG - 2:
                    att_load()
            P.barrier()
            if stop == "G":
                break

            pos[0] = o_RM
            ysl = [V(A(16 * 512 * 4), 8192) for _ in range(2)]
            sqo = [[V(A(1024), 512, BF16) for _ in range(16)] for _ in range(2)]
            rso = [V(A(2048), 512) for _ in range(2)]
            NXR = 6
            xres = [V(A(2048), 512) for _ in range(NXR)]
            rsq = [V(A(2048), 512) for _ in range(2)]
            assert (pos[0] - o_RM) * 4 <= RM_BYTES, ((pos[0] - o_RM) * 4, RM_BYTES)
            if stop == "O1":
                P.barrier()
                break
            ostate = {"bank": 0, "xi": 0, "xl": 0, "defer": []}

            def o_group(s, j):
                sp = s % 2
                wj = next_block()
                bank = ostate["bank"]
                ostate["bank"] = (bank + 1) % 6

                def mm(e):
                    for kt in range(NKT):
                        ins = e.matmul(ps[bank][:, :], lhsT=WB[wj][:, kt * 128:(kt + 1) * 128],
                                       rhs=HT[kt][:, s * 512:(s + 1) * 512], start=(kt == 0), stop=(kt == NKT - 1))
                    return ins
                P.add("tensor", mm, reads=["wb%d" % wj] + ["hT%d.%d" % (kt, s) for kt in range(NKT)], writes=["pb%d" % bank])
                P.add("scalar", lambda e: e.copy(out=ysl[sp][:, j * 512:(j + 1) * 512], in_=ps[bank][:, :]),
                      reads=["pb%d" % bank], writes=["ysl%d_%d" % (sp, j)])
                P.add("scalar", lambda e: e.activation(out=sqo[sp][j], in_=ps[bank][:, :], func=AF.Square),
                      reads=["pb%d" % bank], writes=["sqo%d_%d" % (sp, j)])

            def o_stats(s):
                sp = s % 2

                def mso(e):
                    for j in range(16):
                        ins = e.matmul(ps[7][:, :], lhsT=ones, rhs=sqo[sp][j], start=(j == 0), stop=(j == 15))
                    return ins
                P.add("tensor", mso, reads=["sqo%d_%d" % (sp, j) for j in range(16)] + ["ones"], writes=["pb7"])
                P.add("scalar", lambda e: e.activation(out=rso[sp], in_=ps[7][:, :], func=AF.Ln, bias=EPS, scale=1.0 / DM),
                      reads=["pb7"], writes=["rso%d" % sp])
                P.add("scalar", lambda e: e.activation(out=rso[sp], in_=rso[sp], func=AF.Exp, scale=-0.5),
                      reads=["rso%d" % sp], writes=["rso%d" % sp])

            def o_xload(n, src_d=src_d):
                if n >= 64 or n < ostate["xl"]:
                    return
                ostate["xl"] = n + 1
                s_, j_ = n // 16, n % 16
                xb_ = n % NXR
                P.add("sync", lambda e: e.dma_start(out=xres[xb_], in_=src_d[j_ * 128:(j_ + 1) * 128, s_ * 512:(s_ + 1) * 512]),
                      reads=[XR[j_][s_]], writes=["xres%d" % xb_], slot="xres%d" % xb_)

            def o_pre(s, l=l):
                sp = s % 2

                def msp(e):
                    for j in range(16):
                        ins = e.matmul(ps[6][:, :], lhsT=ones, rhs=sqo[sp][j], start=(j == 0), stop=(j == 15))
                    return ins
                P.add("tensor", msp, reads=["sqo%d_%d" % (sp, j) for j in range(16)] + ["ones"], writes=["pb6"])
                P.add("scalar", lambda e: e.activation(out=rsq[sp], in_=ps[6][:, :], func=AF.Ln, bias=EPS, scale=1.0 / DM),
                      reads=["pb6"], writes=["rsq%d" % sp])
                P.add("scalar", lambda e: e.activation(out=rsq[sp], in_=rsq[sp], func=AF.Exp, scale=-0.5),
                      reads=["rsq%d" % sp], writes=["rsq%d" % sp])
                for j in range(16):
                    ostate["defer"].append(lambda j=j: P.add("vector", lambda e: e.scalar_tensor_tensor(
                        out=HT[j][:, s * 512:(s + 1) * 512], in0=ysl[sp][:, j * 512:(j + 1) * 512], scalar=gcol(l + 1, j),
                        op0=ALU.mult, in1=rsq[sp], op1=ALU.mult),
                        reads=["ysl%d_%d" % (sp, j), "rsq%d" % sp, "gains"], writes=["hT%d.%d" % (j, s)]))

            def o_epi(s, j, l=l, src_d=src_d, dst_d=dst_d):
                if ostate["xi"] == 0:
                    for n_ in range(NXR - 1):
                        o_xload(n_)
                sp = s % 2
                xb = ostate["xi"] % NXR
                ostate["xi"] += 1
                xr = XR[j][s]
                yv = ysl[sp][:, j * 512:(j + 1) * 512]
                yr = "ysl%d_%d" % (sp, j)
                o_xload(ostate["xi"] + NXR - 2)
                P.add("vector", lambda e: e.tensor_tensor(out=yv, in0=yv, in1=rso[sp], op=ALU.mult),
                      reads=[yr, "rso%d" % sp], writes=[yr])
                P.add("vector", lambda e: e.scalar_tensor_tensor(out=yv, in0=yv, scalar=gcol(l, 16 + j), op0=ALU.mult,
                                                                 in1=xres[xb], op1=ALU.add),
                      reads=[yr, "gains", "xres%d" % xb], writes=[yr])
                P.add("sync", lambda e: e.dma_start(out=dst_d[j * 128:(j + 1) * 128, s * 512:(s + 1) * 512], in_=yv),
                      reads=[yr], writes=[xr], slot="xst%d_%d" % (sp, j))
                if l + 1 < depth:
                    P.add("scalar", lambda e: e.activation(out=sqo[sp][j], in_=yv, func=AF.Square),
                          reads=[yr], writes=["sqo%d_%d" % (sp, j)])
                    if j == 15:
                        o_pre(s)

            for s in range(4):
                for j in range(16):
                    for _ in range(2):
                        if ostate["defer"]:
                            ostate["defer"].pop(0)()
                    o_group(s, j)
                    if s > 0 and stop != "O2":
                        if j == 0:
                            for n_ in range(NXR - 1):
                                o_xload((s - 1) * 16 + n_)
                        if j == 1:
                            o_stats(s - 1)
                        if 5 <= j <= 12:
                            o_epi(s - 1, 2 * (j - 5))
                            o_epi(s - 1, 2 * (j - 5) + 1)
            if stop != "O2":
                o_stats(3)
                for j in range(16):
                    o_epi(3, j)
            while ostate["defer"]:
                ostate["defer"].pop(0)()
            P.barrier()

        P.add("sync", None, reads=[XR[j][s] for j in range(16) for s in range(4)])
        P.finalize(st)
    return nc


def _half_cols(half):
    AQ, AK, AV, AG = 3088, 3088 + 1024, 3088 + 2048, 3088 + 3072
    blocks = [None] * NBLK
    for al in range(NHA):
        a = NHA * half + al
        for pos, base in ((0, AQ), (4, AK), (8, AV), (12, AG)):
            blocks[pos + al] = np.arange(base + a * 128, base + (a + 1) * 128)
    for hl in range(NHG):
        h = NHG * half + hl
        blocks[16 + hl] = np.arange(0 + h * 128, 0 + (h + 1) * 128)
        blocks[18 + hl] = np.arange(512 + h * 128, 512 + (h + 1) * 128)
        for d in range(2):
            blocks[20 + 2 * hl + d] = np.arange(1024 + h * 256 + d * 128, 1024 + h * 256 + (d + 1) * 128)
            blocks[24 + 2 * hl + d] = np.arange(2048 + h * 256 + d * 128, 2048 + h * 256 + (d + 1) * 128)
    return np.concatenate(blocks)


def _prep_shared(w_in, w_out, g_pre, g_post, layers):
    nl = len(layers)
    w_ga = np.empty((nl, 128, 256), np.float32)
    w_outr = np.empty((nl * 16, 128, 2048), np.float32)
    for i, l in enumerate(layers):
        Wg = np.asarray(w_in[l])[:, 3072:3088].reshape(16, 128, 16)
        w_ga[i] = Wg.transpose(1, 0, 2).reshape(128, 256)
        Wo = np.asarray(w_out[l]).reshape(16, 128, 16, 128)
        w_outr[i * 16:(i + 1) * 16] = Wo.transpose(2, 1, 0, 3).reshape(16, 128, 2048)
    return {"w_ga": w_ga, "w_outr": w_outr}


def _prep_half(w_in, g_pre, g_post, w_alpha, b_alpha, g_gla, g_att, rel_bias, layers, half):
    nl = len(layers)
    cols = _half_cols(half)
    w_main = np.empty((nl * NBLK, 128, 2048), np.float32)
    gains = np.empty((128, nl * NG), np.float32)
    wal = np.zeros((nl, 32, 256), np.float32)
    biasT = np.empty((nl * NHA, 128, 256), np.float32)
    k = np.arange(128)[:, None]
    q = np.arange(128)[None, :]
    idx0 = np.clip(q - k, -128, 128) + 128
    idx1 = np.clip(128 + q - k, -128, 128) + 128
    for i, l in enumerate(layers):
        W = np.asarray(w_in[l])
        Wm = W[:, cols].reshape(16, 128, NBLK, 128)
        w_main[i * NBLK:(i + 1) * NBLK] = Wm.transpose(2, 1, 0, 3).reshape(NBLK, 128, 2048)
        g0 = i * NG
        gains[:, g0 + 0:g0 + 16] = np.asarray(g_pre[l]).reshape(16, 128).T
        gains[:, g0 + 16:g0 + 32] = np.asarray(g_post[l]).reshape(16, 128).T
        gains[:, g0 + 32:g0 + 36] = np.asarray(g_att[l]).reshape(8, 128)[NHA * half:NHA * (half + 1)].T
        gains[:, g0 + 36:g0 + 40] = np.asarray(g_gla[l]).reshape(8, 128)[4 * half:4 * (half + 1)].T
        rb = np.asarray(rel_bias[l])
        gains[:, g0 + 40:g0 + 44] = np.broadcast_to(rb[NHA * half:NHA * (half + 1), 256][None, :], (128, NHA))
        wal[i, 0:16] = np.asarray(w_alpha[l])[:, 256 * half:256 * (half + 1)]
        wal[i, 16] = np.asarray(b_alpha[l])[256 * half:256 * (half + 1)]
        for al in range(NHA):
            h = NHA * half + al
            biasT[i * NHA + al, :, 0:128] = rb[h][idx0]
            biasT[i * NHA + al, :, 128:256] = rb[h][idx1]
    return {"w_main": w_main, "gains": gains, "wal": wal, "biasT": biasT}


_NC_CACHE = {}
NCORES = 8


def _get_nc(depth):
    if depth not in _NC_CACHE:
        _NC_CACHE[depth] = build(depth)
    return _NC_CACHE[depth]


def kernel(x, w_in, w_out, g_pre, g_post, w_alpha, b_alpha, g_gla, g_att, rel_bias):
    x = np.asarray(x, np.float32)
    B = x.shape[0]
    layers = list(range(DEPTH))
    nc = _get_nc(DEPTH)
    shared = _prep_shared(w_in, w_out, g_pre, g_post, layers)
    halves = [_prep_half(w_in, g_pre, g_post, w_alpha, b_alpha, g_gla, g_att, rel_bias, layers, hf) for hf in range(2)]
    in_maps = []
    for c in range(NCORES):
        m = dict(shared)
        m.update(halves[c % 2])
        m["xT"] = np.ascontiguousarray(x[c // 2].T)
        in_maps.append(m)
    res = run_bass_kernel_spmd(nc, in_maps, core_ids=list(range(NCORES)))
    out = np.stack([np.asarray(res.results[2 * b]["outT"], np.float32).T for b in range(B)], axis=0)
    return np.ascontiguousarray(out.astype(np.float32))
```
